# Optimizing a Trainium2 kernel written in Bass

```python
import math
import jax, jax.numpy as jnp
from jax import lax
import numpy as np

D_MODEL = 1024
BATCH = 8
SEQ = 4096
DEPTH = 4

N_A_LAYERS = DEPTH // 2
N_B_LAYERS = DEPTH - N_A_LAYERS

HEAD_DIM = 64
SB_HEADS = D_MODEL // HEAD_DIM
SWA_Q_HEADS = D_MODEL // HEAD_DIM
SWA_KV_HEADS = SWA_Q_HEADS // 8
SWA_GROUP = SWA_Q_HEADS // SWA_KV_HEADS
WINDOW = 128
BLOCK = 128
D_FF = 4 * D_MODEL
NUM_BUCKETS = 32
MAX_EXACT = NUM_BUCKETS // 2
MAX_DISTANCE = 128
EPS = 1e-5
NEG_INF = -1e30

kernel_name = "yoco_stickbreaking_swa_sinks_trunk"


def rmsnorm(x, g):
    xf = x.astype(jnp.float32)
    y = xf * lax.rsqrt(jnp.mean(xf * xf, axis=-1, keepdims=True) + EPS)
    return (y * g.astype(jnp.float32)).astype(x.dtype)


def sq_relu_mlp(x, w_up, w_down):
    u = jax.nn.relu(x @ w_up)
    return (u * u) @ w_down


def stick_breaking_attention(x, w_qkv, w_o):
    b, s_len, _ = x.shape
    nb = s_len // BLOCK
    scale = 1.0 / math.sqrt(HEAD_DIM)
    q, k, v = jnp.split(x @ w_qkv, 3, axis=-1)
    q = q.reshape(b, nb, BLOCK, SB_HEADS, HEAD_DIM).transpose(1, 0, 3, 2, 4)
    k = k.reshape(b, s_len, SB_HEADS, HEAD_DIM).transpose(0, 2, 1, 3)
    v = v.reshape(b, s_len, SB_HEADS, HEAD_DIM).transpose(0, 2, 1, 3)
    key_pos = jnp.arange(s_len)

    def one_block(args):
        q_blk, blk = args
        z = jnp.einsum('bhqd,bhkd->bhqk', q_blk, k, preferred_element_type=jnp.float32) * scale
        q_pos = blk * BLOCK + jnp.arange(BLOCK)
        causal = key_pos[None, :] < q_pos[:, None]
        log_1m = jnp.where(causal, jax.nn.log_sigmoid(-z), 0.0)
        after = lax.cumsum(log_1m, axis=3, reverse=True) - log_1m
        w = jnp.where(causal, jnp.exp(jax.nn.log_sigmoid(z) + after), 0.0)
        return jnp.einsum('bhqk,bhkd->bhqd', w.astype(v.dtype), v)

    o = lax.map(one_block, (q, jnp.arange(nb)))
    o = o.transpose(1, 0, 3, 2, 4).reshape(b, s_len, SB_HEADS * HEAD_DIM)
    return o @ w_o


def t5_bucket(n):
    nf = jnp.maximum(n, 1).astype(jnp.float32)
    large = MAX_EXACT + (jnp.log(nf / MAX_EXACT) / math.log(MAX_DISTANCE / MAX_EXACT)
                         * (NUM_BUCKETS - MAX_EXACT)).astype(jnp.int32)
    large = jnp.minimum(large, NUM_BUCKETS - 1)
    return jnp.where(n < MAX_EXACT, n, large)


def band_distance():
    qi = jnp.arange(BLOCK)[:, None]
    kj = jnp.arange(2 * BLOCK)[None, :]
    return qi + BLOCK - kj


def relative_bias_band(rel_bias):
    bucket = t5_bucket(jnp.maximum(band_distance(), 0))
    bias = rel_bias.astype(jnp.float32)[bucket]
    return bias.transpose(2, 0, 1).reshape(SWA_KV_HEADS, SWA_GROUP, BLOCK, 2 * BLOCK)


def band_valid(nb):
    dist = band_distance()
    key_pos = jnp.arange(nb)[:, None, None] * BLOCK - BLOCK + jnp.arange(2 * BLOCK)[None, None, :]
    return (dist >= 0) & (dist < WINDOW) & (key_pos >= 0)


def to_band(t):
    b, s_len, g, dh = t.shape
    nb = s_len // BLOCK
    prev = jnp.pad(t, ((0, 0), (BLOCK, 0), (0, 0), (0, 0)))[:, :s_len].reshape(b, nb, BLOCK, g, dh)
    cur = t.reshape(b, nb, BLOCK, g, dh)
    return jnp.concatenate([prev, cur], axis=2)


def shared_kv(h, kv_norm, w_kv, b_kv):
    b, s_len, _ = h.shape
    kv = rmsnorm(h, kv_norm) @ w_kv + b_kv
    k, v = jnp.split(kv, 2, axis=-1)
    k = k.reshape(b, s_len, SWA_KV_HEADS, HEAD_DIM)
    v = v.reshape(b, s_len, SWA_KV_HEADS, HEAD_DIM)
    return to_band(k), to_band(v)


def swa_sink_attention(x, k_band, v_band, w_q, b_q, sinks, w_o, b_o, bias_band, valid):
    b, s_len, _ = x.shape
    nb = s_len // BLOCK
    scale = 1.0 / math.sqrt(HEAD_DIM)
    q = (x @ w_q + b_q).reshape(b, nb, BLOCK, SWA_KV_HEADS, SWA_GROUP, HEAD_DIM)
    s = jnp.einsum('bnqgrd,bnkgd->bngrqk', q, k_band, preferred_element_type=jnp.float32) * scale
    s = s + bias_band[None, None]
    s = jnp.where(valid[None, :, None, None], s, NEG_INF)
    sink = jnp.broadcast_to(sinks.astype(jnp.float32).reshape(SWA_KV_HEADS, SWA_GROUP, 1, 1),
                            s.shape[:-1] + (1,))
    p = jax.nn.softmax(jnp.concatenate([s, sink], axis=-1), axis=-1)[..., :-1]
    o = jnp.einsum('bngrqk,bnkgd->bnqgrd', p.astype(v_band.dtype), v_band)
    return o.reshape(b, s_len, SWA_Q_HEADS * HEAD_DIM) @ w_o + b_o


def setup_inputs(seed: int = 0) -> dict:
    key = jax.random.key(seed)
    ks = jax.random.split(key, 20)
    f32 = jnp.float32

    def nrm(k, shape, fan_in):
        return jax.random.normal(k, shape, f32) * fan_in ** -0.5

    def gain(k, shape):
        return 1.0 + 0.02 * jax.random.normal(k, shape, f32)

    kv_width = 2 * SWA_KV_HEADS * HEAD_DIM
    return {
        "x": jax.random.normal(ks[0], (BATCH, SEQ, D_MODEL), f32),
        "a_norm": gain(ks[1], (N_A_LAYERS, D_MODEL)),
        "a_wqkv": nrm(ks[2], (N_A_LAYERS, D_MODEL, 3 * SB_HEADS * HEAD_DIM), D_MODEL),
        "a_wo": nrm(ks[3], (N_A_LAYERS, SB_HEADS * HEAD_DIM, D_MODEL), SB_HEADS * HEAD_DIM),
        "kv_norm": gain(ks[4], (D_MODEL,)),
        "w_kv": nrm(ks[5], (D_MODEL, kv_width), D_MODEL),
        "b_kv": 0.02 * jax.random.normal(ks[6], (kv_width,), f32),
        "b_norm": gain(ks[7], (N_B_LAYERS, D_MODEL)),
        "b_wq": nrm(ks[8], (N_B_LAYERS, D_MODEL, SWA_Q_HEADS * HEAD_DIM), D_MODEL),
        "b_bq": 0.02 * jax.random.normal(ks[9], (N_B_LAYERS, SWA_Q_HEADS * HEAD_DIM), f32),
        "b_sinks": 0.5 * jax.random.normal(ks[10], (N_B_LAYERS, SWA_Q_HEADS), f32),
        "b_wo": nrm(ks[11], (N_B_LAYERS, SWA_Q_HEADS * HEAD_DIM, D_MODEL), SWA_Q_HEADS * HEAD_DIM),
        "b_bo": 0.02 * jax.random.normal(ks[12], (N_B_LAYERS, D_MODEL), f32),
        "rel_bias": 0.5 * jax.random.normal(ks[13], (NUM_BUCKETS, SWA_Q_HEADS), f32),
        "mlp_norm": gain(ks[14], (DEPTH, D_MODEL)),
        "mlp_up": nrm(ks[15], (DEPTH, D_MODEL, D_FF), D_MODEL),
        "mlp_down": nrm(ks[16], (DEPTH, D_FF, D_MODEL), D_FF),
        "final_norm": gain(ks[17], (D_MODEL,)),
    }


def reference(x, a_norm, a_wqkv, a_wo, kv_norm, w_kv, b_kv, b_norm, b_wq, b_bq, b_sinks,
              b_wo, b_bo, rel_bias, mlp_norm, mlp_up, mlp_down, final_norm):
    nb = x.shape[1] // BLOCK
    bias_band = relative_bias_band(rel_bias)
    valid = band_valid(nb)
    h = x
    k_band = v_band = None
    for layer in range(DEPTH):
        if layer < N_A_LAYERS:
            h = h + stick_breaking_attention(rmsnorm(h, a_norm[layer]), a_wqkv[layer], a_wo[layer])
        else:
            j = layer - N_A_LAYERS
            if j == 0:
                k_band, v_band = shared_kv(h, kv_norm, w_kv, b_kv)
            h = h + swa_sink_attention(rmsnorm(h, b_norm[j]), k_band, v_band, b_wq[j], b_bq[j],
                                       b_sinks[j], b_wo[j], b_bo[j], bias_band, valid)
        h = h + sq_relu_mlp(rmsnorm(h, mlp_norm[layer]), mlp_up[layer], mlp_down[layer])
    return rmsnorm(h, final_norm)
```

```python
import numpy as np
from contextlib import ExitStack
import concourse.bass as bass
import concourse.mybir as mybir
from concourse.bass_utils import run_bass_kernel_spmd

F32, BF16 = mybir.dt.float32, mybir.dt.bfloat16
AF = mybir.ActivationFunctionType
ALU = mybir.AluOpType

D = 1024
SEQ = 4096
TILE = 512
DFF = 4096
NH = 16
EPS = 1e-5
NEG = -30000.0
NSLAB = 3


class T:
    __slots__ = ("name", "w", "r")

    def __init__(self, name):
        self.name = name
        self.w = None
        self.r = {}


class Prog:
    ENGS = ("pe", "act", "dve", "pool", "sp")

    def __init__(self):
        self.ops = {e: [] for e in self.ENGS}
        self.seen = {e: {} for e in self.ENGS}
        self.dma_val = {}

    def op(self, eng, fn, reads=(), writes=(), sig=True, dma=None, chain=True):
        deps = {}
        self.seq = getattr(self, "seq", 0) + 1

        def add(tok):
            if tok is None:
                return
            key = tok[:2]
            if deps.get(key, -1) < tok[2]:
                deps[key] = tok[2]

        for t in reads:
            add(t.w)
        for t in writes:
            if chain:
                add(t.w)
            for tok in t.r.values():
                add(tok)
        waits = []
        seen = self.seen[eng]
        for key, val in deps.items():
            if key[0] == "e" and key[1] == eng and eng in ("pe",):
                continue
            if seen.get(key, -1) >= val:
                continue
            seen[key] = val
            waits.append((key[0], key[1], val))
        idx = len(self.ops[eng])
        if dma is not None:
            v = self.dma_val.get(dma, 0) + 16
            self.dma_val[dma] = v
            tok = ("d", dma, v)
        else:
            tok = ("e", eng, idx)
        self.ops[eng].append((waits, fn, sig and dma is None, dma, self.seq))
        for t in reads:
            t.r[tok[:2]] = tok
        for t in writes:
            t.w = tok
            t.r = {}
        return tok

    def emit(self, nc, block, sems, dsems):
        cnt = {}
        for e in self.ENGS:
            c = 0
            arr = []
            for (_, _, sig, _, sq) in self.ops[e]:
                if sig:
                    c += 1
                arr.append((c, sig, sq))
            res = [None] * len(arr)
            nxt = None
            for i in range(len(arr) - 1, -1, -1):
                if arr[i][1]:
                    nxt = (arr[i][0], arr[i][2])
                res[i] = nxt
            cnt[e] = res
        regs = {"pe": block.tensor, "act": block.scalar, "dve": block.vector,
                "pool": block.gpsimd, "sp": block.sync}
        final = dict(self.dma_val)

        def make(e):
            def body(eng):
                for (waits, fn, sig, dma, sq) in self.ops[e]:
                    for (k, name, val) in waits:
                        if k == "e":
                            v = cnt[name][val]
                            assert v is not None, (e, name, val)
                            assert v[1] < sq, ("deadlock risk", e, name, val, v, sq)
                            eng.wait_ge(sems[name], v[0])
                        else:
                            eng.wait_ge(dsems[name], val)
                    ins = fn(eng)
                    if dma is not None:
                        ins.then_inc(dsems[dma], 16)
                    elif sig:
                        ins.then_inc(sems[e], 1)
                if e == "pool":
                    for name, val in final.items():
                        eng.wait_ge(dsems[name], val)
            return body

        for e in self.ENGS:
            regs[e](make(e))


def build(nc, n_tiles=8, n_a=2, n_b=2, dbg=False):
    S = n_tiles * TILE
    P = Prog()
    es = ExitStack()

    def dram(name, shape, dt, kind):
        return nc.dram_tensor(name, list(shape), dt, kind=kind)

    x_d = dram("x", [S, D], F32, "ExternalInput")
    out_d = dram("out", [S, D], F32, "ExternalOutput")
    wsrc = {
        "a_wqkv": dram("a_wqkv", [2, D, 3 * D], F32, "ExternalInput"),
        "a_wo": dram("a_wo", [2, D, D], F32, "ExternalInput"),
        "w_kv": dram("w_kv", [D, 256], F32, "ExternalInput"),
        "b_wq": dram("b_wq", [2, D, D], F32, "ExternalInput"),
        "b_wo": dram("b_wo", [2, D, D], F32, "ExternalInput"),
        "mlp_up": dram("mlp_up", [4, D, DFF], F32, "ExternalInput"),
        "mlp_down": dram("mlp_down", [4, DFF, D], F32, "ExternalInput"),
    }
    gT_d = dram("gT", [128, 9, 8], F32, "ExternalInput")
    gF_d = dram("gF", [1, D], F32, "ExternalInput")
    bq_d = dram("bqT", [128, 2, 8], F32, "ExternalInput")
    bk_d = dram("bk_dup", [128, 2], F32, "ExternalInput")
    bv_d = dram("bv", [1, 128], F32, "ExternalInput")
    bo_d = dram("bo", [1, 2 * D], F32, "ExternalInput")
    sk_d = dram("sinks", [1, 32], F32, "ExternalInput")
    rb_d = dram("rel_bias", [32, 16], F32, "ExternalInput")
    oh_d = dram("onehot", [32, 128], F32, "ExternalInput")

    def slab_shape(v):
        sh = list(v.shape)
        if len(sh) == 2:
            return sh
        return [sh[0], sh[1] // 1024, sh[2] // 512, 128, 8, 512]

    wbf = {k: dram(k + "_bf", slab_shape(v), BF16, "Internal") for k, v in wsrc.items()}
    wbf_t = {}
    KTs = [dram("KTs%d" % l, [8, 128, S], BF16, "Internal") for l in range(2)]
    Vs = [dram("Vs%d" % l, [S, D], BF16, "Internal") for l in range(2)]
    KTs_t = [[T("KTs%d_%d" % (l, p)) for p in range(8)] for l in range(2)]
    Vs_t = [[[T("Vs%d_%d_%d" % (l, hf, tb)) for tb in range(4)] for hf in range(2)] for l in range(2)]
    G_d = dram("Gscr", [16, 128, 384], F32, "Internal")
    G_t = T("Gscr")

    def sb(name, shape, dt):
        return es.enter_context(nc.sbuf_tensor(name, list(shape), dt))

    psum_all = es.enter_context(nc.psum_tensor("psum_all", [128, 4096], F32))
    banks = [psum_all[:, i * 512:(i + 1) * 512] for i in range(8)]
    bank_t = [T("bank%d" % i) for i in range(8)]

    h = [sb("h%d" % i, [128, 4, D], F32) for i in range(1)]
    h_t = [T("h0")]
    xhat = [sb("xhat%d" % i, [128, D], BF16) for i in range(2)]
    xhat_t = [T("xhat0"), T("xhat1")]
    stat = sb("stat", [128, 32], F32)
    stat_ts = [T("stat%d" % i) for i in range(4)]
    stat_tsX = [T("statX%d" % i) for i in range(4)]
    xnT = sb("xnT", [128, 8, TILE], BF16)
    xnT_t = T("xnT")
    big = [sb("big%d" % i, [128, 8, TILE], BF16) for i in range(2)]
    big_t = [T("big%d" % i) for i in range(2)]
    QT, OT = big
    QT_t, OT_t = big_t
    kvs = [sb("kvs%d" % i, [128, 512], BF16) for i in range(4)]
    kvs_t = [T("kvs%d" % i) for i in range(4)]
    relu_b = [sb("relu%d" % i, [128, 512], F32) for i in range(2)]
    relu_t = [T("relu0"), T("relu1")]
    xs = sb("xs", [128, D], F32)
    xs_t = T("xs")
    xnT_X = sb("xnT_X", [128, 8, TILE], BF16)
    xnT_X_t = T("xnT_X")
    QT_X = sb("QT_X", [128, 8, TILE], BF16)
    QT_X_t = T("QT_X")
    OT_X = sb("OT_X", [128, 8, TILE], BF16)
    OT_X_t = T("OT_X")
    slab = [sb("slab%d" % i, [128, 8, 512], BF16) for i in range(NSLAB)]
    slab_t = [T("slab%d" % i) for i in range(NSLAB)]
    NE, NSP, NEX, NA = 3, 3, 2, 2
    E2 = [sb("E2_%d" % i, [128, 1024], F32) for i in range(NE)]
    E2_t = [T("E2_%d" % i) for i in range(NE)]
    SP2 = [sb("SP2_%d" % i, [128, 1024], BF16) for i in range(NSP)]
    SP2_t = [T("SP2_%d" % i) for i in range(NSP)]
    EX2 = [sb("EX2_%d" % i, [128, 1024], F32) for i in range(NEX)]
    EX2_t = [T("EX2_%d" % i) for i in range(NEX)]
    A2 = [sb("A2_%d" % i, [128, 1024], BF16) for i in range(NA)]
    A2_t = [T("A2_%d" % i) for i in range(NA)]
    Ebuf = [E2[0][:, 0:512]]
    E_t = [E2_t[0]]
    KTh = [sb("KTh%d" % i, [128, S], BF16) for i in range(2)]
    KTh_t = [T("KTh0"), T("KTh1")]
    Vh = [sb("Vh%d" % i, [128, S // 128, 128], BF16) for i in range(2)]
    Vh_t = [T("Vh0"), T("Vh1")]
    Rb, Rb_t = Ebuf[0], E_t[0]
    ident = sb("ident", [128, 128], BF16)
    tri = sb("tri", [128, 128], BF16)
    omt = sb("omt", [128, 128], BF16)
    negm = sb("negm", [128, 128], BF16)
    cst_t = T("consts")
    gT = sb("gTs", [128, 9, 8], F32)
    gF = sb("gFs", [128, D], F32)
    bq = sb("bqs", [128, 2, 8], F32)
    bk = sb("bks", [128, 2], F32)
    bv = sb("bvs", [128, 128], F32)
    bo = sb("bos", [128, D], F32)
    bo_t = T("bo")
    ES = sb("ESs", [128, 32], F32)
    EB = sb("EBs", [128, 2, 16, 128], F32)
    EB_t = T("EB")
    KB = [sb("KB%d" % g, [128, 640], BF16) for g in range(2)]
    KB_t = [T("KB0"), T("KB1")]
    VB = sb("VB", [128, 5, 2, 80], BF16)
    VB_t = T("VB")
    PT = [sb("PT%d" % i, [128, 2, 16, 128], BF16) for i in range(1)]
    PT_t = [T("PT0")]
    Otok = sb("Otok", [128, D], BF16)
    Otok_t = T("Otok")
    junk, junk_t = Otok, Otok_t
    den = sb("den", [128, 32], F32)
    den_t = T("den")

    def bf_bank(i):
        return banks[i][:].bitcast(BF16)

    def mm(bank, out, lhsT, rhs, start, stop, reads, sig=None):
        P.op("pe", lambda e: e.matmul(out, lhsT, rhs, start=start, stop=stop, skip_group_check=True),
             reads=reads, writes=[bank_t[bank]], sig=(stop if sig is None else sig))

    def act(out, in_, func, reads, writes, scale=1.0, bias=0.0, accum=None):
        def fn(e):
            kw = {}
            if accum is not None:
                kw["accum_out"] = accum
            return e.activation(out, in_, func, bias=bias, scale=scale, **kw)
        P.op("act", fn, reads=reads, writes=writes)

    def tt(eng, out, in0, in1, op, reads, writes):
        P.op(eng, lambda e: e.tensor_tensor(out, in0, in1, op), reads=reads, writes=writes)

    def dma(eng, out, in_, sem, reads, writes, chain=True, **kw):
        P.op(eng, lambda e: e.dma_start(out, in_, **kw), reads=reads, writes=writes, dma=sem, chain=chain)

    def setup_consts():
        w = [cst_t]
        P.op("pool", lambda e: e.memset(junk[:, 0:512], 1.0), writes=[junk_t])
        P.op("pool", lambda e: e.memset(Rb[:], 0.0), writes=[Rb_t])
        P.op("pool", lambda e: e.affine_select(ident[:], junk[:, 0:128], [[-1, 128]], ALU.is_equal, 0.0,
                                               base=0, channel_multiplier=1), reads=[junk_t], writes=w)
        P.op("pool", lambda e: e.affine_select(tri[:], junk[:, 0:128], [[-1, 128]], ALU.is_ge, 0.0,
                                               base=0, channel_multiplier=1), reads=[junk_t], writes=w)
        P.op("pool", lambda e: e.affine_select(omt[:], junk[:, 0:128], [[1, 128]], ALU.is_gt, 0.0,
                                               base=0, channel_multiplier=-1), reads=[junk_t], writes=w)
        P.op("pool", lambda e: e.affine_select(negm[:], Rb[:].bitcast(BF16)[:, 0:128], [[1, 128]],
                                               ALU.is_gt, NEG, base=0, channel_multiplier=-1),
             reads=[Rb_t], writes=w)
        dma("sp", gT[:], gT_d.ap(), "cst", [], w)
        dma("sp", bq[:], bq_d.ap(), "cst", [], w)
        dma("sp", bk[:], bk_d.ap(), "cst", [], w)
        dma("sp", gF[:], gF_d.ap()[0].partition_broadcast(128), "cst", [], w)
        dma("sp", bv[:], bv_d.ap()[0].partition_broadcast(128), "cst", [], w)
        dma("sp", ES[:], sk_d.ap()[0].partition_broadcast(128), "cst", [], w)
        if n_b > 0:
            act(ES[:], ES[:], AF.Exp, [cst_t], w)
            dma("sp", Ebuf[0][0:32, 0:16], rb_d.ap(), "cst2", [], [E_t[0]])
            dma("sp", Ebuf[0][0:32, 128:256], oh_d.ap(), "cst2", [], [E_t[0]])
            P.op("pe", lambda e: e.matmul(banks[5][0:16, 0:128], Ebuf[0][0:32, 0:16], Ebuf[0][0:32, 128:256],
                                          start=True, stop=True), reads=[E_t[0]], writes=[bank_t[5]])
            P.op("dve", lambda e: e.memset(E2[1][0:16, 0:384], 0.0), writes=[E2_t[1]])
            act(E2[1][0:16, 128:256], banks[5][0:16, 0:128], AF.Exp, [bank_t[5]], [E2_t[1]])
            dma("pool", G_d.ap(), bass.AP(E2[1], 0, [[1024, 16], [0, 128], [1, 384]]), "cst3", [E2_t[1]], [G_t])
            for part in range(2):
                off = 128 + 128 * (1 - part)
                src = bass.AP(G_d, off, [[383, 128], [128 * 384, 16], [1, 128]])
                dma("pool", EB[:, part, :, :], src, "cst4", [G_t], [EB_t])

    def cast_weight(name, layer=None):
        key = (name, layer)
        if key in wbf_t:
            return wbf_t[key]
        t = T("wbf_%s_%s" % (name, layer))
        wbf_t[key] = t
        src = wsrc[name].ap()
        dst = wbf[name].ap()
        sem = "cast_%s_%s" % (name, layer)
        if layer is None:
            R, C = src.shape
            for r0 in range(0, R, 512):
                dma("pool", dst[r0:r0 + 512, :], src[r0:r0 + 512, :], sem, [], [t], chain=False)
        else:
            src = src[layer]
            R, C = src.shape
            for kb in range(R // 1024):
                for sc in range(C // 512):
                    dma("pool", dst[layer, kb, sc],
                        src[kb * 1024:(kb + 1) * 1024, sc * 512:(sc + 1) * 512].rearrange("(kc p) n -> p kc n", p=128),
                        sem, [], [t], chain=False)
        return t

    slab_state = {"n": 0}

    def load_slab(pieces, wt):
        i = slab_state["n"]
        slab_state["n"] += 1
        b = i % NSLAB
        for (c0, ncol, src) in pieces:
            dma("sp", slab[b][:, :, c0:c0 + ncol], src.rearrange("(kc p) n -> p kc n", p=128),
                "slab%d" % b, [wt], [slab_t[b]])
        return b

    def wslab(name, layer, r0, c0):
        wt = cast_weight(name, layer)
        i = slab_state["n"]
        slab_state["n"] += 1
        b = i % NSLAB
        dma("sp", slab[b][:], wbf[name].ap()[layer, r0 // 1024, c0 // 512], "slab%d" % b, [wt], [slab_t[b]])
        return b

    rot = {"g": 0}
    ring = {"n0": 0}

    def gbank(pool=(5, 6, 7)):
        b = pool[rot["g"] % len(pool)]
        rot["g"] += 1
        return b

    cp = {"n": 0}

    def evac(out, in_, reads, writes, eng=None):
        if eng is None:
            eng = "dve" if cp["n"] % 2 == 0 else "act"
            cp["n"] += 1
        if eng == "dve":
            P.op("dve", lambda e: e.tensor_copy(out, in_), reads=reads, writes=writes)
        else:
            act(out, in_, AF.Copy, reads, writes)

    def norm_T(hb, gi, dst=None, dst_t=None, xsrc=None):
        if dst is None:
            dst, dst_t = xnT, xnT_t
        gv = bass.AP(gT, gi * 8, [[72, 128], [1, 8], [0, 128]])
        so = 0 if xsrc is None else 16
        sts = stat_ts if xsrc is None else stat_tsX

        def src(tb):
            return (h[hb][:, tb, :], h_t[hb]) if xsrc is None else (xs[:], xs_t)

        def ss(tb):
            a, at = src(tb)
            P.op("dve", lambda e: e.scalar_tensor_tensor(junk[:], a, 1.0, a, ALU.mult, ALU.mult,
                                                         accum_out=stat[:, so + tb:so + tb + 1]),
                 reads=[at], writes=[junk_t, sts[tb]])

        def rs(tb):
            act(stat[:, so + 4 + tb:so + 5 + tb], stat[:, so + tb:so + tb + 1], AF.Ln, [sts[tb]], [sts[tb]],
                scale=1.0 / D, bias=EPS)
            act(stat[:, so + 8 + tb:so + 9 + tb], stat[:, so + 4 + tb:so + 5 + tb], AF.Exp, [sts[tb]], [sts[tb]],
                scale=-0.5)

        def ts(tb):
            a, at = src(tb)
            xb = tb % 2
            P.op("dve", lambda e: e.tensor_scalar(xhat[xb][:], a, stat[:, so + 8 + tb:so + 9 + tb], None, ALU.mult),
                 reads=[at, sts[tb]], writes=[xhat_t[xb]])

        def tr(tb):
            xb = tb % 2
            tpb = (7, 6)[tb % 2]
            tpv = bf_bank(tpb)
            for kc in range(8):
                P.op("pe", lambda e, kc=kc: e.transpose(tpv[:, kc * 128:(kc + 1) * 128],
                                                        xhat[xb][:, kc * 128:(kc + 1) * 128], ident[:]),
                     reads=[xhat_t[xb], cst_t], writes=[bank_t[tpb]], sig=(kc == 7))

        def ev(tb):
            tpb = (7, 6)[tb % 2]
            tpv = bf_bank(tpb)
            tt("dve", dst[:, :, tb * 128:(tb + 1) * 128], tpv.rearrange("p (k t) -> p k t", k=8), gv, ALU.mult,
               [bank_t[tpb], cst_t], [dst_t])

        if xsrc is None:
            for tb in range(4):
                ss(tb)
            for tb in range(4):
                rs(tb)
            ts(0); tr(0); ts(1); tr(1); ev(0); ts(2); tr(2); ev(1); ts(3); tr(3); ev(2); ev(3)
        else:
            for tb in range(4):
                r0 = xsrc * TILE + tb * 128
                dma("pool", xs[:], x_d.ap()[r0:r0 + 128, :], "xs", [], [xs_t])
                ss(tb); rs(tb); ts(tb); tr(tb); ev(tb)

    def proj_featmajor(sb_idx, col0, dst, dst_t, dst_chunk, bias=None, src=None, src_t=None):
        if src is None:
            src, src_t = xnT, xnT_t
        b = gbank()
        for kc in range(8):
            mm(b, banks[b][:], slab[sb_idx][:, kc, col0:col0 + 128], src[:, kc, :], kc == 0, kc == 7,
               [slab_t[sb_idx], src_t])
        out = dst[:, dst_chunk, :] if dst_chunk is not None else dst
        if bias is None:
            evac(out, banks[b][:], [bank_t[b]], [dst_t])
        else:
            act(out, banks[b][:], AF.Identity, [bank_t[b], cst_t], [dst_t], bias=bias)

    def resid_proj(hb, src, src_t, name, layer):
        for half in range(2):
            s = wslab(name, layer, 0, half * 512)
            for tb in range(4):
                b = gbank()
                for kc in range(8):
                    mm(b, banks[b][:], src[:, kc, tb * 128:(tb + 1) * 128], slab[s][:, kc, :], kc == 0, kc == 7,
                       [src_t, slab_t[s]])
                hv = h[hb][:, tb, half * 512:(half + 1) * 512]
                tt("dve", hv, banks[b][:], hv, ALU.add, [bank_t[b], h_t[hb]], [h_t[hb]])

    xstate = {"g": None, "acc": 0.0, "ratio": 0.0}

    def tick():
        g = xstate["g"]
        if g is None or g.done():
            return
        xstate["acc"] += xstate["ratio"]
        while xstate["acc"] >= 1.0 and not g.done():
            g.step(True)
            xstate["acc"] -= 1.0

    def mlp(hb, L):
        norm_T(hb, 2 + L)
        DB = (5, 6)
        for fh in range(2):
            for s4 in range(4):
                s = wslab("mlp_up", L, 0, (fh * 4 + s4) * 512)
                for jj in range(4):
                    jl = s4 * 4 + jj
                    b = gbank()
                    for kc in range(8):
                        mm(b, banks[b][:], slab[s][:, kc, jj * 128:(jj + 1) * 128], xnT[:, kc, :], kc == 0, kc == 7,
                           [slab_t[s], xnT_t])
                    rb = jl % 2
                    act(relu_b[rb][:], banks[b][:], AF.Relu, [bank_t[b]], [relu_t[rb]])
                    tt("dve", big[jl // 8][:, jl % 8, :], banks[b][:], relu_b[rb][:], ALU.mult,
                       [bank_t[b], relu_t[rb]], [big_t[jl // 8]])
                    tick()
            for half in range(2):
                for tbp in range(2):
                    for jg in range(2):
                        s = wslab("mlp_down", L, (fh * 16 + jg * 8) * 128, half * 512)
                        for t2 in range(2):
                            tb = tbp * 2 + t2
                            b = DB[t2]
                            for j in range(8):
                                mm(b, banks[b][:], big[jg][:, j, tb * 128:(tb + 1) * 128], slab[s][:, j, :],
                                   jg == 0 and j == 0, jg == 1 and j == 7, [big_t[jg], slab_t[s]])
                            tick()
                    for t2 in range(2):
                        tb = tbp * 2 + t2
                        hv = h[hb][:, tb, half * 512:(half + 1) * 512]
                        tt("dve", hv, banks[DB[t2]][:], hv, ALU.add, [bank_t[DB[t2]], h_t[hb]], [h_t[hb]])
        g = xstate["g"]
        if g is not None:
            g.drain()

    kvr = {"n": 0}

    def a_front(c, l, hb, ctx):
        xn_c, xn_ct, QT_c, QT_ct = ctx
        if ctx is CTX_X:
            norm_T(hb, l, dst=xn_c, dst_t=xn_ct, xsrc=c)
        else:
            norm_T(hb, l, dst=xn_c, dst_t=xn_ct)
        for s6 in range(4):
            s = wslab("a_wqkv", l, 0, s6 * 512)
            for pp in range(4):
                pair = (s6 % 2) * 4 + pp
                if s6 < 2:
                    proj_featmajor(s, pp * 128, QT_c, QT_ct, pair, src=xn_c, src_t=xn_ct)
                else:
                    r = kvr["n"] % 4
                    kvr["n"] += 1
                    proj_featmajor(s, pp * 128, kvs[r][:], kvs_t[r], None, src=xn_c, src_t=xn_ct)
                    dma("sp", KTs[l].ap()[pair, :, c * TILE:(c + 1) * TILE], kvs[r][:], "kvs%d" % r,
                        [kvs_t[r]], [KTs_t[l][pair]])
        for half in range(2):
            s = wslab("a_wqkv", l, 0, 2048 + half * 512)
            for tb in range(4):
                b = gbank()
                for kc in range(8):
                    mm(b, banks[b][:], xn_c[:, kc, tb * 128:(tb + 1) * 128], slab[s][:, kc, :], kc == 0, kc == 7,
                       [xn_ct, slab_t[s]])
                r = kvr["n"] % 4
                kvr["n"] += 1
                evac(kvs[r][:], banks[b][:], [bank_t[b]], [kvs_t[r]])
                r0 = c * TILE + tb * 128
                dma("sp", Vs[l].ap()[r0:r0 + 128, half * 512:(half + 1) * 512], kvs[r][:], "kvs%d" % r,
                    [kvs_t[r]], [Vs_t[l][half][tb]])

    class Attn:
        def __init__(self, c, l, QT_c, QT_ct, OT_c, OT_ct, zring, obanks):
            self.c, self.l = c, l
            self.QT, self.QT_t, self.OT, self.OT_t = QT_c, QT_ct, OT_c, OT_ct
            self.zring, self.obanks = zring, obanks
            self.nkb = 4 * c + 4
            self.steps = [(pair, kb) for pair in range(8) for kb in range(self.nkb - 1, -1, -1)]
            self.nxt = 0
            self.it = 0
            self.fly = {}
            self.kv_load(0, True)
            self.kv_load(1, True)

        def n_iters(self):
            return len(self.steps) + 3

        def done(self):
            return self.nxt >= len(self.steps) and not self.fly

        def kv_load(self, pair, hist):
            c, l = self.c, self.l
            hbuf = pair % 2
            b0, b1 = (0, 4 * c) if hist else (4 * c, 4 * c + 4)
            if b1 == b0:
                return
            dma("sp", KTh[hbuf][:, b0 * 128:b1 * 128], KTs[l].ap()[pair, :, b0 * 128:b1 * 128], "kld%d" % hbuf,
                [KTs_t[l][pair]], [KTh_t[hbuf]], chain=hist)
            dma("sp", Vh[hbuf][:, b0:b1, :],
                Vs[l].ap()[b0 * 128:b1 * 128, pair * 128:(pair + 1) * 128].rearrange("(b p) f -> p b f", p=128),
                "vld%d" % hbuf, Vs_t[l][pair // 4], [Vh_t[hbuf]], chain=hist)

        @staticmethod
        def v2(ap2, c0):
            if c0 == 0:
                return ap2
            return ap2.rearrange("p (h n) -> p h n", h=2)[:, :, c0:512]

        def col0(self, kb):
            di = kb - 4 * self.c
            return 128 * di if di > 0 else 0

        def s_z(self, i):
            c, l, nkb = self.c, self.l, self.nkb
            pair, kb = self.steps[i]
            hbuf = pair % 2
            if kb == nkb - 1:
                if pair >= 2:
                    self.kv_load(pair, True)
                self.kv_load(pair, False)
            k = ring["n0"]
            ring["n0"] += 1
            z0 = self.zring[k % len(self.zring)]
            idx = (k % NE, k % NSP, k % NEX, k % NA)
            ei, si, xi, ai = idx
            di = kb - 4 * c
            c0 = self.col0(kb)
            for hh in range(2):
                base = hh * 64
                zb = z0 + hh
                mm(zb, banks[zb][:, c0:512], KTh[hbuf][base:base + 64, kb * 128:(kb + 1) * 128],
                   self.QT[base:base + 64, pair, c0:512], True, di < 0, [KTh_t[hbuf], self.QT_t],
                   sig=(di < 0 and hh == 1))
                if di >= 0:
                    dc = 128 * di
                    mm(zb, banks[zb][:, dc:dc + 128], ident[:], negm[:], False, True, [cst_t], sig=(hh == 1))
            zz = psum_all[:, z0 * 512:(z0 + 2) * 512]
            act(self.v2(E2[ei][:], c0), self.v2(zz, c0), AF.Exp, [bank_t[z0], bank_t[z0 + 1]], [E2_t[ei]],
                scale=0.125)
            act(self.v2(SP2[si][:], c0), self.v2(E2[ei][:], c0), AF.Ln, [E2_t[ei]], [SP2_t[si]], bias=1.0)
            return idx

        def s_tri(self, i, idx):
            pair, kb = self.steps[i]
            ei, si, xi, ai = idx
            c0 = self.col0(kb)
            for hh in range(2):
                mm(hh, banks[hh][:, c0:512], tri[:], SP2[si][:, hh * 512 + c0:(hh + 1) * 512], kb == self.nkb - 1,
                   False, [cst_t, SP2_t[si]], sig=(hh == 1))
            act(self.v2(EX2[xi][:], c0), self.v2(psum_all[:, 0:1024], c0), AF.Exp, [bank_t[0], bank_t[1]],
                [EX2_t[xi]], scale=-1.0)

        def s_omt(self, i, idx):
            pair, kb = self.steps[i]
            ei, si, xi, ai = idx
            c0 = self.col0(kb)
            if kb > 0:
                for hh in range(2):
                    mm(hh, banks[hh][:, c0:512], omt[:], SP2[si][:, hh * 512 + c0:(hh + 1) * 512], False, False,
                       [cst_t, SP2_t[si]], sig=(hh == 1))
            tt("dve", self.v2(A2[ai][:], c0), self.v2(E2[ei][:], c0), self.v2(EX2[xi][:], c0), ALU.mult,
               [E2_t[ei], EX2_t[xi]], [A2_t[ai]])

        def s_av(self, i, idx):
            pair, kb = self.steps[i]
            ei, si, xi, ai = idx
            hbuf = pair % 2
            ob = self.obanks[pair % len(self.obanks)]
            c0 = self.col0(kb)
            for hh in range(2):
                base = hh * 64
                mm(ob, banks[ob][base:base + 64, c0:512], Vh[hbuf][:, kb, base:base + 64],
                   A2[ai][:, hh * 512 + c0:(hh + 1) * 512], kb == self.nkb - 1, kb == 0, [Vh_t[hbuf], A2_t[ai]],
                   sig=(hh == 1))
            if kb == 0:
                evac(self.OT[:, pair, :], banks[ob][:], [bank_t[ob]], [self.OT_t])

        def step(self, allow_new):
            it = self.it
            if allow_new and self.nxt < len(self.steps):
                i = self.nxt
                self.nxt += 1
                self.fly[i] = (it, self.s_z(i))
            for i in sorted(self.fly):
                t0, idx = self.fly[i]
                if it - t0 == 2:
                    self.s_omt(i, idx)
            for i in sorted(self.fly):
                t0, idx = self.fly[i]
                if it - t0 == 1:
                    self.s_tri(i, idx)
            for i in sorted(self.fly):
                t0, idx = self.fly[i]
                if it - t0 == 3:
                    self.s_av(i, idx)
                    del self.fly[i]
            self.it += 1

        def drain(self):
            while self.nxt < len(self.steps) and self.steps[self.nxt][1] != self.nkb - 1:
                self.step(True)
            while self.fly:
                self.step(False)

        def finish(self):
            while not self.done():
                self.step(True)

    CTX_Y = (xnT, xnT_t, QT, QT_t)
    CTX_X = (xnT_X, xnT_X_t, QT_X, QT_X_t)

    def shared_kv(c, hb):
        norm_T(hb, 8)
        wt = cast_weight("w_kv")
        a = wbf["w_kv"].ap()
        s = load_slab([(0, 64, a[:, 0:64]), (64, 64, a[:, 0:64]), (128, 64, a[:, 64:128]),
                       (192, 64, a[:, 64:128]), (256, 128, a[:, 128:256])], wt)
        if c > 0:
            for g in range(2):
                P.op("pool", lambda e, g=g: e.tensor_copy(KB[g][:, 0:128], KB[g][:, 512:640]),
                     reads=[KB_t[g]], writes=[KB_t[g]])
            P.op("pool", lambda e: e.tensor_copy(VB[:, 0, :, :], VB[:, 4, :, :]), reads=[VB_t], writes=[VB_t])
        for g in range(2):
            b = gbank()
            for kc in range(8):
                mm(b, banks[b][:], slab[s][:, kc, g * 128:(g + 1) * 128], xnT[:, kc, :], kc == 0, kc == 7,
                   [slab_t[s], xnT_t])
            act(KB[g][:, 128:640], banks[b][:], AF.Identity, [bank_t[b], cst_t], [KB_t[g]], bias=bk[:, g:g + 1])
        for tb in range(4):
            b = gbank()
            for kc in range(8):
                mm(b, banks[b][:, 0:128], xnT[:, kc, tb * 128:(tb + 1) * 128], slab[s][:, kc, 256:384],
                   kc == 0, kc == 7, [xnT_t, slab_t[s]])
            tt("dve", VB[:, tb + 1, :, 0:64], banks[b][:, 0:128].rearrange("p (g d) -> p g d", g=2),
               bv[:].rearrange("p (g d) -> p g d", g=2), ALU.add, [bank_t[b], cst_t], [VB_t])

    def layer_b(c, hb, j):
        norm_T(hb, 6 + j)
        for s2 in range(2):
            s = wslab("b_wq", j, 0, s2 * 512)
            for pp in range(4):
                pair = s2 * 4 + pp
                proj_featmajor(s, pp * 128, QT, QT_t, pair, bias=bq[:, j, pair:pair + 1])

        def parts_of(tb):
            return [1] + ([0] if 4 * c + tb > 0 else [])

        def stage_a(tb):
            PTb, PTb_t = PT[tb % len(PT)], PT_t[tb % len(PT)]
            for part in parts_of(tb):
                kc0 = (tb + part) * 128
                for hg in range(4):
                    g8, par = hg // 2, hg % 2
                    base = par * 64
                    b = 5 + par
                    b2 = 3 + par
                    for hi in range(4):
                        hd = g8 * 8 + par + 2 * hi
                        mm(b, banks[b][:, hi * 128:(hi + 1) * 128], KB[g8][base:base + 64, kc0:kc0 + 128],
                           QT[base:base + 64, hd // 2, tb * 128:(tb + 1) * 128], True, True,
                           [KB_t[g8], QT_t], sig=(hi == 3))
                    h0 = g8 * 8 + par
                    act(banks[b2][:], banks[b][:], AF.Exp, [bank_t[b]], [bank_t[b2]], scale=0.125)
                    tt("dve", PTb[:, part, h0:h0 + 7:2, :], banks[b2][:].rearrange("p (a t) -> p a t", a=4),
                       EB[:, part, h0:h0 + 7:2, :], ALU.mult, [bank_t[b2], EB_t], [PTb_t])

        def stage_b(tb):
            PTb, PTb_t = PT[tb % len(PT)], PT_t[tb % len(PT)]
            parts = parts_of(tb)
            for hd in range(16):
                g = hd // 8
                ob = hd // 7
                oc = (hd % 7) * 66
                for pi, part in enumerate(parts):
                    mm(ob, banks[ob][:, oc:oc + 66], PTb[:, part, hd, :], VB[:, tb + part, g, 0:66],
                       pi == 0, pi == len(parts) - 1, [PTb_t, VB_t],
                       sig=(pi == len(parts) - 1 and (hd % 7 == 6 or hd == 15)))
            for ob in range(3):
                nh = 7 if ob < 2 else 2
                bv3 = banks[ob][:, 0:nh * 66].rearrange("p (a d) -> p a d", d=66)
                tt("dve", den[:, 7 * ob:7 * ob + nh], bv3[:, :, 64], ES[:, j * 16 + 7 * ob:j * 16 + 7 * ob + nh],
                   ALU.add, [bank_t[ob], cst_t], [den_t])
            P.op("dve", lambda e: e.reciprocal(den[:, 16:32], den[:, 0:16]), reads=[den_t], writes=[den_t])
            for ob in range(3):
                nh = 7 if ob < 2 else 2
                bv3 = banks[ob][:, 0:nh * 66].rearrange("p (a d) -> p a d", d=66)
                rv = bass.AP(den, 16 + 7 * ob, [[32, 128], [1, nh], [0, 64]])
                tt("dve", Otok[:, 7 * ob * 64:(7 * ob + nh) * 64].rearrange("p (a d) -> p a d", d=64),
                   bv3[:, :, 0:64], rv, ALU.mult, [bank_t[ob], den_t], [Otok_t])
            tpv = bf_bank(7)
            for kc in range(8):
                P.op("pe", lambda e, kc=kc: e.transpose(tpv[:, kc * 128:(kc + 1) * 128],
                                                        Otok[:, kc * 128:(kc + 1) * 128], ident[:]),
                     reads=[Otok_t, cst_t], writes=[bank_t[7]], sig=(kc == 7))
            evac(OT[:, :, tb * 128:(tb + 1) * 128], tpv.rearrange("p (k t) -> p k t", k=8), [bank_t[7]], [OT_t])

        for tb in range(4):
            stage_a(tb)
            stage_b(tb)
        dma("pool", bo[:], bo_d.ap()[0][j * D:(j + 1) * D].partition_broadcast(128), "bo", [], [bo_t])
        for tb in range(4):
            P.op("pool", lambda e, tb=tb: e.tensor_tensor(h[hb][:, tb, :], h[hb][:, tb, :], bo[:], ALU.add),
                 reads=[h_t[hb], bo_t], writes=[h_t[hb]])
        resid_proj(hb, OT, OT_t, "b_wo", j)

    def final_out(c, hb, do_norm=True):
        for tb in range(4):
            stat_t = stat_ts[tb]
            if do_norm:
                P.op("dve", lambda e, tb=tb: e.scalar_tensor_tensor(junk[:], h[hb][:, tb, :], 1.0, h[hb][:, tb, :],
                                                                    ALU.mult, ALU.mult,
                                                                    accum_out=stat[:, tb:tb + 1]),
                     reads=[h_t[hb]], writes=[junk_t, stat_t])
                act(stat[:, 4 + tb:5 + tb], stat[:, tb:tb + 1], AF.Ln, [stat_t], [stat_t], scale=1.0 / D, bias=EPS)
                act(stat[:, 8 + tb:9 + tb], stat[:, 4 + tb:5 + tb], AF.Exp, [stat_t], [stat_t], scale=-0.5)
                P.op("dve", lambda e, tb=tb: e.scalar_tensor_tensor(h[hb][:, tb, :], h[hb][:, tb, :],
                                                                    stat[:, 8 + tb:9 + tb], gF[:],
                                                                    ALU.mult, ALU.mult),
                     reads=[h_t[hb], stat_t, cst_t], writes=[h_t[hb]])
                dma("pool", out_d.ap()[c * TILE + tb * 128:c * TILE + (tb + 1) * 128, :], h[hb][:, tb, :], "out",
                    [h_t[hb]], [])
            else:
                dma("pool", out_d.ap()[c * TILE + tb * 128:c * TILE + (tb + 1) * 128, :], h[hb][:, tb, :], "out",
                    [h_t[hb]], [])

    setup_consts()
    dma("pool", h[0][:], x_d.ap()[0:TILE, :].rearrange("(tb p) f -> p tb f", p=128), "xld", [], [h_t[0]])
    P.op("pool", lambda e: e.memset(VB[:], 0.0), writes=[VB_t])
    P.op("pool", lambda e: e.memset(VB[:, :, :, 64:65], 1.0), writes=[VB_t])

    def x_start(c):
        a_front(c, 0, 0, CTX_X)
        return Attn(c, 0, QT_X, QT_X_t, OT_X, OT_X_t, zring=(3,), obanks=(2,))

    def cast_all():
        for l in range(n_a):
            for nm in ("a_wqkv", "a_wo", "mlp_up", "mlp_down"):
                cast_weight(nm, l)
        for j in range(n_b):
            if j == 0:
                cast_weight("w_kv")
            cast_weight("b_wq", j)
            cast_weight("b_wo", j)
            cast_weight("mlp_up", 2 + j)
            cast_weight("mlp_down", 2 + j)

    n_mlp = n_a + n_b
    if n_a > 0:
        cast_weight("a_wqkv", 0)
        g0 = x_start(0)
        cast_all()
        g0.finish()
    else:
        cast_all()
    for c in range(n_tiles):
        hb = 0
        if c > 0:
            dma("pool", h[0][:], x_d.ap()[c * TILE:(c + 1) * TILE, :].rearrange("(tb p) f -> p tb f", p=128),
                "xld", [], [h_t[0]])
        xstate["g"] = None
        if n_a > 0:
            resid_proj(hb, OT_X, OT_X_t, "a_wo", 0)
            if c + 1 < n_tiles:
                g = x_start(c + 1)
                xstate.update(g=g, acc=0.0, ratio=g.n_iters() / (0.9 * 64 * n_mlp))
            mlp(hb, 0)
            for l in range(1, n_a):
                a_front(c, l, hb, CTX_Y)
                Attn(c, l, QT, QT_t, OT, OT_t, zring=(3, 5), obanks=(2, 7)).finish()
                resid_proj(hb, OT, OT_t, "a_wo", l)
                mlp(hb, l)
        for j in range(n_b):
            if j == 0:
                shared_kv(c, hb)
            layer_b(c, hb, j)
            mlp(hb, 2 + j)
        if xstate["g"] is not None:
            xstate["g"].finish()
        final_out(c, hb, do_norm=not dbg)

    if dbg:
        print("sbuf bytes remaining", nc.sbuf_bytes_remaining)
    sem_names = ["pe", "act", "dve", "pool"]
    sems = {n: es.enter_context(nc.semaphore("sem_" + n)) for n in sem_names}
    dsems = {n: es.enter_context(nc.semaphore("dsem_" + n)) for n in P.dma_val}
    with nc.Block() as block:
        P.emit(nc, block, sems, dsems)
    es.close()
    return nc


def _bucket_onehot():
    n = np.arange(128)
    nf = np.maximum(n, 1).astype(np.float32)
    large = 16 + (np.log(nf / np.float32(16)) / np.float32(np.log(128 / 16)) * np.float32(16)).astype(np.int32)
    large = np.minimum(large, 31)
    bucket = np.where(n < 16, n, large)
    oh = np.zeros((32, 128), np.float32)
    oh[bucket, n] = 1.0
    return oh


def make_in_maps(inp, n_cores, S):
    f = lambda a: np.ascontiguousarray(np.asarray(a, dtype=np.float32))
    g_all = np.stack([inp["a_norm"][0], inp["a_norm"][1], inp["mlp_norm"][0], inp["mlp_norm"][1],
                      inp["mlp_norm"][2], inp["mlp_norm"][3], inp["b_norm"][0], inp["b_norm"][1],
                      inp["kv_norm"]], 0)
    gT = f(np.asarray(g_all).reshape(9, 8, 128).transpose(2, 0, 1))
    bqT = f(np.asarray(inp["b_bq"]).reshape(2, 8, 128).transpose(2, 0, 1))
    bkv = np.asarray(inp["b_kv"])
    bk_dup = f(np.stack([np.concatenate([bkv[0:64], bkv[0:64]]), np.concatenate([bkv[64:128], bkv[64:128]])], 1))
    shared = {
        "a_wqkv": f(inp["a_wqkv"]), "a_wo": f(inp["a_wo"]), "w_kv": f(inp["w_kv"]), "b_wq": f(inp["b_wq"]),
        "b_wo": f(inp["b_wo"]), "mlp_up": f(inp["mlp_up"]), "mlp_down": f(inp["mlp_down"]),
        "gT": gT, "gF": f(np.asarray(inp["final_norm"]).reshape(1, D)), "bqT": bqT, "bk_dup": bk_dup,
        "bv": f(bkv[128:256].reshape(1, 128)), "bo": f(np.asarray(inp["b_bo"]).reshape(1, 2 * D)),
        "sinks": f(np.asarray(inp["b_sinks"]).reshape(1, 32)), "rel_bias": f(inp["rel_bias"]),
        "onehot": _bucket_onehot(),
    }
    x = np.asarray(inp["x"], dtype=np.float32)
    maps = []
    for b in range(n_cores):
        m = dict(shared)
        m["x"] = np.ascontiguousarray(x[b, :S])
        maps.append(m)
    return maps


def kernel(**inputs):
    nc = bass.Bass("TRN2", target_bir_lowering=False)
    build(nc)
    in_maps = make_in_maps(inputs, 8, SEQ)
    res = run_bass_kernel_spmd(nc, in_maps, core_ids=list(range(8)))
    return np.stack([np.asarray(r["out"]) for r in res.results], 0).astype(np.float32)
```

```python
import numpy as np
from contextlib import ExitStack
import concourse.bass as bass
import concourse.mybir as mybir
from concourse.bass_utils import run_bass_kernel_spmd

F32, BF16 = mybir.dt.float32, mybir.dt.bfloat16
AF = mybir.ActivationFunctionType
ALU = mybir.AluOpType

D = 1024
SEQ = 4096
TILE = 512
DFF = 4096
NH = 16
EPS = 1e-5
NEG = -30000.0
NSLAB = 3


class T:
    __slots__ = ("name", "w", "r")

    def __init__(self, name):
        self.name = name
        self.w = None
        self.r = {}


class Prog:
    ENGS = ("pe", "act", "dve", "pool", "sp")

    def __init__(self):
        self.ops = {e: [] for e in self.ENGS}
        self.seen = {e: {} for e in self.ENGS}
        self.dma_val = {}

    def op(self, eng, fn, reads=(), writes=(), sig=True, dma=None, chain=True):
        deps = {}
        self.seq = getattr(self, "seq", 0) + 1

        def add(tok):
            if tok is None:
                return
            key = tok[:2]
            if deps.get(key, -1) < tok[2]:
                deps[key] = tok[2]

        for t in reads:
            add(t.w)
        for t in writes:
            if chain:
                add(t.w)
            for tok in t.r.values():
                add(tok)
        waits = []
        seen = self.seen[eng]
        for key, val in deps.items():
            if key[0] == "e" and key[1] == eng and eng in ("pe",):
                continue
            if seen.get(key, -1) >= val:
                continue
            seen[key] = val
            waits.append((key[0], key[1], val))
        idx = len(self.ops[eng])
        if dma is not None:
            v = self.dma_val.get(dma, 0) + 16
            self.dma_val[dma] = v
            tok = ("d", dma, v)
        else:
            tok = ("e", eng, idx)
        self.ops[eng].append((waits, fn, sig and dma is None, dma, self.seq))
        for t in reads:
            t.r[tok[:2]] = tok
        for t in writes:
            t.w = tok
            t.r = {}
        return tok

    def emit(self, nc, block, sems, dsems):
        cnt = {}
        for e in self.ENGS:
            c = 0
            arr = []
            for (_, _, sig, _, sq) in self.ops[e]:
                if sig:
                    c += 1
                arr.append((c, sig, sq))
            res = [None] * len(arr)
            nxt = None
            for i in range(len(arr) - 1, -1, -1):
                if arr[i][1]:
                    nxt = (arr[i][0], arr[i][2])
                res[i] = nxt
            cnt[e] = res
        regs = {"pe": block.tensor, "act": block.scalar, "dve": block.vector,
                "pool": block.gpsimd, "sp": block.sync}
        final = dict(self.dma_val)

        def make(e):
            def body(eng):
                for (waits, fn, sig, dma, sq) in self.ops[e]:
                    for (k, name, val) in waits:
                        if k == "e":
                            v = cnt[name][val]
                            assert v is not None, (e, name, val)
                            assert v[1] < sq, ("deadlock risk", e, name, val, v, sq)
                            eng.wait_ge(sems[name], v[0])
                        else:
                            eng.wait_ge(dsems[name], val)
                    ins = fn(eng)
                    if dma is not None:
                        ins.then_inc(dsems[dma], 16)
                    elif sig:
                        ins.then_inc(sems[e], 1)
                if e == "pool":
                    for name, val in final.items():
                        eng.wait_ge(dsems[name], val)
            return body

        for e in self.ENGS:
            regs[e](make(e))


def build(nc, n_tiles=8, n_a=2, n_b=2, dbg=False):
    S = n_tiles * TILE
    P = Prog()
    es = ExitStack()

    def dram(name, shape, dt, kind):
        return nc.dram_tensor(name, list(shape), dt, kind=kind)

    x_d = dram("x", [S, D], F32, "ExternalInput")
    out_d = dram("out", [S, D], F32, "ExternalOutput")
    wsrc = {
        "a_wqkv": dram("a_wqkv", [2, D, 3 * D], F32, "ExternalInput"),
        "a_wo": dram("a_wo", [2, D, D], F32, "ExternalInput"),
        "w_kv": dram("w_kv", [D, 256], F32, "ExternalInput"),
        "b_wq": dram("b_wq", [2, D, D], F32, "ExternalInput"),
        "b_wo": dram("b_wo", [2, D, D], F32, "ExternalInput"),
        "mlp_up": dram("mlp_up", [4, D, DFF], F32, "ExternalInput"),
        "mlp_down": dram("mlp_down", [4, DFF, D], F32, "ExternalInput"),
    }
    gT_d = dram("gT", [128, 9, 8], F32, "ExternalInput")
    gF_d = dram("gF", [1, D], F32, "ExternalInput")
    bq_d = dram("bqT", [128, 2, 8], F32, "ExternalInput")
    bk_d = dram("bk_dup", [128, 2], F32, "ExternalInput")
    bv_d = dram("bv", [1, 128], F32, "ExternalInput")
    bo_d = dram("bo", [1, 2 * D], F32, "ExternalInput")
    sk_d = dram("sinks", [1, 32], F32, "ExternalInput")
    rb_d = dram("rel_bias", [32, 16], F32, "ExternalInput")
    oh_d = dram("onehot", [32, 128], F32, "ExternalInput")

    def slab_shape(v):
        sh = list(v.shape)
        if len(sh) == 2:
            return sh
        return [sh[0], sh[1] // 1024, sh[2] // 512, 128, 8, 512]

    wbf = {k: dram(k + "_bf", slab_shape(v), BF16, "Internal") for k, v in wsrc.items()}
    wbf_t = {}
    KTs = [dram("KTs%d" % l, [8, 128, S], BF16, "Internal") for l in range(2)]
    Vs = [dram("Vs%d" % l, [S, D], BF16, "Internal") for l in range(2)]
    KTs_t = [[T("KTs%d_%d" % (l, p)) for p in range(8)] for l in range(2)]
    Vs_t = [[[T("Vs%d_%d_%d" % (l, hf, tb)) for tb in range(4)] for hf in range(2)] for l in range(2)]
    G_d = dram("Gscr", [16, 128, 384], F32, "Internal")
    G_t = T("Gscr")

    def sb(name, shape, dt):
        return es.enter_context(nc.sbuf_tensor(name, list(shape), dt))

    psum_all = es.enter_context(nc.psum_tensor("psum_all", [128, 4096], F32))
    banks = [psum_all[:, i * 512:(i + 1) * 512] for i in range(8)]
    bank_t = [T("bank%d" % i) for i in range(8)]

    h = [sb("h%d" % i, [128, 4, D], F32) for i in range(1)]
    h_t = [T("h0")]
    xhat = [sb("xhat%d" % i, [128, D], BF16) for i in range(2)]
    xhat_t = [T("xhat0"), T("xhat1")]
    stat = sb("stat", [128, 32], F32)
    stat_ts = [T("stat%d" % i) for i in range(4)]
    stat_tsX = [T("statX%d" % i) for i in range(4)]
    xnT = sb("xnT", [128, 8, TILE], BF16)
    xnT_t = T("xnT")
    big = [sb("big%d" % i, [128, 8, TILE], BF16) for i in range(2)]
    big_t = [T("big%d" % i) for i in range(2)]
    QT, OT = big
    QT_t, OT_t = big_t
    kvs = [sb("kvs%d" % i, [128, 512], BF16) for i in range(4)]
    kvs_t = [T("kvs%d" % i) for i in range(4)]
    relu_b = [sb("relu%d" % i, [128, 512], F32) for i in range(2)]
    relu_t = [T("relu0"), T("relu1")]
    xs = sb("xs", [128, D], F32)
    xs_t = T("xs")
    xnT_X = sb("xnT_X", [128, 8, TILE], BF16)
    xnT_X_t = T("xnT_X")
    QT_X = sb("QT_X", [128, 8, TILE], BF16)
    QT_X_t = T("QT_X")
    OT_X = sb("OT_X", [128, 8, TILE], BF16)
    OT_X_t = T("OT_X")
    slab = [sb("slab%d" % i, [128, 8, 512], BF16) for i in range(NSLAB)]
    slab_t = [T("slab%d" % i) for i in range(NSLAB)]
    NE, NSP, NEX, NA = 3, 3, 2, 2
    E2 = [sb("E2_%d" % i, [128, 1024], F32) for i in range(NE)]
    E2_t = [T("E2_%d" % i) for i in range(NE)]
    SP2 = [sb("SP2_%d" % i, [128, 1024], BF16) for i in range(NSP)]
    SP2_t = [T("SP2_%d" % i) for i in range(NSP)]
    EX2 = [sb("EX2_%d" % i, [128, 1024], F32) for i in range(NEX)]
    EX2_t = [T("EX2_%d" % i) for i in range(NEX)]
    A2 = [sb("A2_%d" % i, [128, 1024], BF16) for i in range(NA)]
    A2_t = [T("A2_%d" % i) for i in range(NA)]
    Ebuf = [E2[0][:, 0:512]]
    E_t = [E2_t[0]]
    KTh = [sb("KTh%d" % i, [128, S], BF16) for i in range(2)]
    KTh_t = [T("KTh0"), T("KTh1")]
    Vh = [sb("Vh%d" % i, [128, S // 128, 128], BF16) for i in range(2)]
    Vh_t = [T("Vh0"), T("Vh1")]
    Rb, Rb_t = Ebuf[0], E_t[0]
    ident = sb("ident", [128, 128], BF16)
    tri = sb("tri", [128, 128], BF16)
    omt = sb("omt", [128, 128], BF16)
    negm = sb("negm", [128, 128], BF16)
    cst_t = T("consts")
    gT = sb("gTs", [128, 9, 8], F32)
    gF = sb("gFs", [128, D], F32)
    bq = sb("bqs", [128, 2, 8], F32)
    bk = sb("bks", [128, 2], F32)
    bv = sb("bvs", [128, 128], F32)
    bo = sb("bos", [128, D], F32)
    bo_t = T("bo")
    ES = sb("ESs", [128, 32], F32)
    EB = sb("EBs", [128, 2, 16, 128], F32)
    EB_t = T("EB")
    KB = [sb("KB%d" % g, [128, 640], BF16) for g in range(2)]
    KB_t = [T("KB0"), T("KB1")]
    VB = sb("VB", [128, 5, 2, 80], BF16)
    VB_t = T("VB")
    PT = [sb("PT%d" % i, [128, 2, 16, 128], BF16) for i in range(1)]
    PT_t = [T("PT0")]
    Otok = sb("Otok", [128, D], BF16)
    Otok_t = T("Otok")
    junk, junk_t = Otok, Otok_t
    den = sb("den", [128, 32], F32)
    den_t = T("den")

    def bf_bank(i):
        return banks[i][:].bitcast(BF16)

    def mm(bank, out, lhsT, rhs, start, stop, reads, sig=None):
        P.op("pe", lambda e: e.matmul(out, lhsT, rhs, start=start, stop=stop, skip_group_check=True),
             reads=reads, writes=[bank_t[bank]], sig=(stop if sig is None else sig))

    def act(out, in_, func, reads, writes, scale=1.0, bias=0.0, accum=None):
        def fn(e):
            kw = {}
            if accum is not None:
                kw["accum_out"] = accum
            return e.activation(out, in_, func, bias=bias, scale=scale, **kw)
        P.op("act", fn, reads=reads, writes=writes)

    def tt(eng, out, in0, in1, op, reads, writes):
        P.op(eng, lambda e: e.tensor_tensor(out, in0, in1, op), reads=reads, writes=writes)

    def dma(eng, out, in_, sem, reads, writes, chain=True, **kw):
        P.op(eng, lambda e: e.dma_start(out, in_, **kw), reads=reads, writes=writes, dma=sem, chain=chain)

    def setup_consts():
        w = [cst_t]
        P.op("pool", lambda e: e.memset(junk[:, 0:512], 1.0), writes=[junk_t])
        P.op("pool", lambda e: e.memset(Rb[:], 0.0), writes=[Rb_t])
        P.op("pool", lambda e: e.affine_select(ident[:], junk[:, 0:128], [[-1, 128]], ALU.is_equal, 0.0,
                                               base=0, channel_multiplier=1), reads=[junk_t], writes=w)
        P.op("pool", lambda e: e.affine_select(tri[:], junk[:, 0:128], [[-1, 128]], ALU.is_ge, 0.0,
                                               base=0, channel_multiplier=1), reads=[junk_t], writes=w)
        P.op("pool", lambda e: e.affine_select(omt[:], junk[:, 0:128], [[1, 128]], ALU.is_gt, 0.0,
                                               base=0, channel_multiplier=-1), reads=[junk_t], writes=w)
        P.op("pool", lambda e: e.affine_select(negm[:], Rb[:].bitcast(BF16)[:, 0:128], [[1, 128]],
                                               ALU.is_gt, NEG, base=0, channel_multiplier=-1),
             reads=[Rb_t], writes=w)
        dma("sp", gT[:], gT_d.ap(), "cst", [], w)
        dma("sp", bq[:], bq_d.ap(), "cst", [], w)
        dma("sp", bk[:], bk_d.ap(), "cst", [], w)
        dma("sp", gF[:], gF_d.ap()[0].partition_broadcast(128), "cst", [], w)
        dma("sp", bv[:], bv_d.ap()[0].partition_broadcast(128), "cst", [], w)
        dma("sp", ES[:], sk_d.ap()[0].partition_broadcast(128), "cst", [], w)
        if n_b > 0:
            act(ES[:], ES[:], AF.Exp, [cst_t], w)
            dma("sp", Ebuf[0][0:32, 0:16], rb_d.ap(), "cst2", [], [E_t[0]])
            dma("sp", Ebuf[0][0:32, 128:256], oh_d.ap(), "cst2", [], [E_t[0]])
            P.op("pe", lambda e: e.matmul(banks[5][0:16, 0:128], Ebuf[0][0:32, 0:16], Ebuf[0][0:32, 128:256],
                                          start=True, stop=True), reads=[E_t[0]], writes=[bank_t[5]])
            P.op("dve", lambda e: e.memset(E2[1][0:16, 0:384], 0.0), writes=[E2_t[1]])
            act(E2[1][0:16, 128:256], banks[5][0:16, 0:128], AF.Exp, [bank_t[5]], [E2_t[1]])
            dma("pool", G_d.ap(), bass.AP(E2[1], 0, [[1024, 16], [0, 128], [1, 384]]), "cst3", [E2_t[1]], [G_t])
            for part in range(2):
                off = 128 + 128 * (1 - part)
                src = bass.AP(G_d, off, [[383, 128], [128 * 384, 16], [1, 128]])
                dma("pool", EB[:, part, :, :], src, "cst4", [G_t], [EB_t])

    def cast_weight(name, layer=None):
        key = (name, layer)
        if key in wbf_t:
            return wbf_t[key]
        t = T("wbf_%s_%s" % (name, layer))
        wbf_t[key] = t
        src = wsrc[name].ap()
        dst = wbf[name].ap()
        sem = "cast_%s_%s" % (name, layer)
        if layer is None:
            R, C = src.shape
            for r0 in range(0, R, 512):
                dma("pool", dst[r0:r0 + 512, :], src[r0:r0 + 512, :], sem, [], [t], chain=False)
        else:
            src = src[layer]
            R, C = src.shape
            for kb in range(R // 1024):
                for sc in range(C // 512):
                    dma("pool", dst[layer, kb, sc],
                        src[kb * 1024:(kb + 1) * 1024, sc * 512:(sc + 1) * 512].rearrange("(kc p) n -> p kc n", p=128),
                        sem, [], [t], chain=False)
        return t

    slab_state = {"n": 0}

    def load_slab(pieces, wt):
        i = slab_state["n"]
        slab_state["n"] += 1
        b = i % NSLAB
        for (c0, ncol, src) in pieces:
            dma("sp", slab[b][:, :, c0:c0 + ncol], src.rearrange("(kc p) n -> p kc n", p=128),
                "slab%d" % b, [wt], [slab_t[b]])
        return b

    def wslab(name, layer, r0, c0):
        wt = cast_weight(name, layer)
        i = slab_state["n"]
        slab_state["n"] += 1
        b = i % NSLAB
        dma("sp", slab[b][:], wbf[name].ap()[layer, r0 // 1024, c0 // 512], "slab%d" % b, [wt], [slab_t[b]])
        return b

    rot = {"g": 0}
    ring = {"n0": 0}

    def gbank(pool=(5, 6, 7)):
        b = pool[rot["g"] % len(pool)]
        rot["g"] += 1
        return b

    cp = {"n": 0}

    def evac(out, in_, reads, writes, eng=None):
        if eng is None:
            eng = "dve" if cp["n"] % 2 == 0 else "act"
            cp["n"] += 1
        if eng == "dve":
            P.op("dve", lambda e: e.tensor_copy(out, in_), reads=reads, writes=writes)
        else:
            act(out, in_, AF.Copy, reads, writes)

    def norm_T(hb, gi, dst=None, dst_t=None, xsrc=None):
        if dst is None:
            dst, dst_t = xnT, xnT_t
        gv = bass.AP(gT, gi * 8, [[72, 128], [1, 8], [0, 128]])
        so = 0 if xsrc is None else 16
        sts = stat_ts if xsrc is None else stat_tsX

        def src(tb):
            return (h[hb][:, tb, :], h_t[hb]) if xsrc is None else (xs[:], xs_t)

        def ss(tb):
            a, at = src(tb)
            P.op("dve", lambda e: e.scalar_tensor_tensor(junk[:], a, 1.0, a, ALU.mult, ALU.mult,
                                                         accum_out=stat[:, so + tb:so + tb + 1]),
                 reads=[at], writes=[junk_t, sts[tb]])

        def rs(tb):
            act(stat[:, so + 4 + tb:so + 5 + tb], stat[:, so + tb:so + tb + 1], AF.Ln, [sts[tb]], [sts[tb]],
                scale=1.0 / D, bias=EPS)
            act(stat[:, so + 8 + tb:so + 9 + tb], stat[:, so + 4 + tb:so + 5 + tb], AF.Exp, [sts[tb]], [sts[tb]],
                scale=-0.5)

        def ts(tb):
            a, at = src(tb)
            xb = tb % 2
            P.op("dve", lambda e: e.tensor_scalar(xhat[xb][:], a, stat[:, so + 8 + tb:so + 9 + tb], None, ALU.mult),
                 reads=[at, sts[tb]], writes=[xhat_t[xb]])

        def tr(tb):
            xb = tb % 2
            tpb = (7, 6)[tb % 2]
            tpv = bf_bank(tpb)
            for kc in range(8):
                P.op("pe", lambda e, kc=kc: e.transpose(tpv[:, kc * 128:(kc + 1) * 128],
                                                        xhat[xb][:, kc * 128:(kc + 1) * 128], ident[:]),
                     reads=[xhat_t[xb], cst_t], writes=[bank_t[tpb]], sig=(kc == 7))

        def ev(tb):
            tpb = (7, 6)[tb % 2]
            tpv = bf_bank(tpb)
            tt("dve", dst[:, :, tb * 128:(tb + 1) * 128], tpv.rearrange("p (k t) -> p k t", k=8), gv, ALU.mult,
               [bank_t[tpb], cst_t], [dst_t])

        if xsrc is None:
            for tb in range(4):
                ss(tb)
            for tb in range(4):
                rs(tb)
            ts(0); tr(0); ts(1); tr(1); ev(0); ts(2); tr(2); ev(1); ts(3); tr(3); ev(2); ev(3)
        else:
            for tb in range(4):
                r0 = xsrc * TILE + tb * 128
                dma("sp", xs[:], x_d.ap()[r0:r0 + 128, :], "xs", [], [xs_t])
                ss(tb); rs(tb); ts(tb); tr(tb); ev(tb)

    def proj_featmajor(sb_idx, col0, dst, dst_t, dst_chunk, bias=None, src=None, src_t=None):
        if src is None:
            src, src_t = xnT, xnT_t
        b = gbank()
        for kc in range(8):
            mm(b, banks[b][:], slab[sb_idx][:, kc, col0:col0 + 128], src[:, kc, :], kc == 0, kc == 7,
               [slab_t[sb_idx], src_t])
        out = dst[:, dst_chunk, :] if dst_chunk is not None else dst
        if bias is None:
            evac(out, banks[b][:], [bank_t[b]], [dst_t])
        else:
            act(out, banks[b][:], AF.Identity, [bank_t[b], cst_t], [dst_t], bias=bias)

    def resid_proj(hb, src, src_t, name, layer):
        for half in range(2):
            s = wslab(name, layer, 0, half * 512)
            for tb in range(4):
                b = gbank()
                for kc in range(8):
                    mm(b, banks[b][:], src[:, kc, tb * 128:(tb + 1) * 128], slab[s][:, kc, :], kc == 0, kc == 7,
                       [src_t, slab_t[s]])
                hv = h[hb][:, tb, half * 512:(half + 1) * 512]
                tt("dve", hv, banks[b][:], hv, ALU.add, [bank_t[b], h_t[hb]], [h_t[hb]])

    xstate = {"g": None, "acc": 0.0, "ratio": 0.0}

    def tick():
        g = xstate["g"]
        if g is None or g.done():
            return
        xstate["acc"] += xstate["ratio"]
        while xstate["acc"] >= 1.0 and not g.done():
            g.step(True)
            xstate["acc"] -= 1.0

    def mlp(hb, L):
        norm_T(hb, 2 + L)
        DB = (5, 6)
        for fh in range(2):
            for s4 in range(4):
                s = wslab("mlp_up", L, 0, (fh * 4 + s4) * 512)
                for jj in range(4):
                    jl = s4 * 4 + jj
                    b = gbank()
                    for kc in range(8):
                        mm(b, banks[b][:], slab[s][:, kc, jj * 128:(jj + 1) * 128], xnT[:, kc, :], kc == 0, kc == 7,
                           [slab_t[s], xnT_t])
                    rb = jl % 2
                    act(relu_b[rb][:], banks[b][:], AF.Relu, [bank_t[b]], [relu_t[rb]])
                    tt("dve", big[jl // 8][:, jl % 8, :], banks[b][:], relu_b[rb][:], ALU.mult,
                       [bank_t[b], relu_t[rb]], [big_t[jl // 8]])
                    tick()
            for half in range(2):
                for tbp in range(2):
                    for jg in range(2):
                        s = wslab("mlp_down", L, (fh * 16 + jg * 8) * 128, half * 512)
                        for t2 in range(2):
                            tb = tbp * 2 + t2
                            b = DB[t2]
                            for j in range(8):
                                mm(b, banks[b][:], big[jg][:, j, tb * 128:(tb + 1) * 128], slab[s][:, j, :],
                                   jg == 0 and j == 0, jg == 1 and j == 7, [big_t[jg], slab_t[s]])
                            tick()
                    for t2 in range(2):
                        tb = tbp * 2 + t2
                        hv = h[hb][:, tb, half * 512:(half + 1) * 512]
                        tt("dve", hv, banks[DB[t2]][:], hv, ALU.add, [bank_t[DB[t2]], h_t[hb]], [h_t[hb]])
        g = xstate["g"]
        if g is not None:
            g.drain()

    kvr = {"n": 0}

    def a_front(c, l, hb, ctx):
        xn_c, xn_ct, QT_c, QT_ct = ctx
        if ctx is CTX_X:
            norm_T(hb, l, dst=xn_c, dst_t=xn_ct, xsrc=c)
        else:
            norm_T(hb, l, dst=xn_c, dst_t=xn_ct)
        for s6 in range(4):
            s = wslab("a_wqkv", l, 0, s6 * 512)
            for pp in range(4):
                pair = (s6 % 2) * 4 + pp
                if s6 < 2:
                    proj_featmajor(s, pp * 128, QT_c, QT_ct, pair, src=xn_c, src_t=xn_ct)
                else:
                    r = kvr["n"] % 4
                    kvr["n"] += 1
                    proj_featmajor(s, pp * 128, kvs[r][:], kvs_t[r], None, src=xn_c, src_t=xn_ct)
                    dma("sp", KTs[l].ap()[pair, :, c * TILE:(c + 1) * TILE], kvs[r][:], "kvs%d" % r,
                        [kvs_t[r]], [KTs_t[l][pair]])
        for half in range(2):
            s = wslab("a_wqkv", l, 0, 2048 + half * 512)
            for tb in range(4):
                b = gbank()
                for kc in range(8):
                    mm(b, banks[b][:], xn_c[:, kc, tb * 128:(tb + 1) * 128], slab[s][:, kc, :], kc == 0, kc == 7,
                       [xn_ct, slab_t[s]])
                r = kvr["n"] % 4
                kvr["n"] += 1
                evac(kvs[r][:], banks[b][:], [bank_t[b]], [kvs_t[r]])
                r0 = c * TILE + tb * 128
                dma("sp", Vs[l].ap()[r0:r0 + 128, half * 512:(half + 1) * 512], kvs[r][:], "kvs%d" % r,
                    [kvs_t[r]], [Vs_t[l][half][tb]])

    class Attn:
        def __init__(self, c, l, QT_c, QT_ct, OT_c, OT_ct, zring, obanks):
            self.c, self.l = c, l
            self.QT, self.QT_t, self.OT, self.OT_t = QT_c, QT_ct, OT_c, OT_ct
            self.zring, self.obanks = zring, obanks
            self.nkb = 4 * c + 4
            self.steps = [(pair, kb) for pair in range(8) for kb in range(self.nkb - 1, -1, -1)]
            self.nxt = 0
            self.it = 0
            self.fly = {}
            self.kv_load(0, True)
            self.kv_load(1, True)

        def n_iters(self):
            return len(self.steps) + 3

        def done(self):
            return self.nxt >= len(self.steps) and not self.fly

        def kv_load(self, pair, hist):
            c, l = self.c, self.l
            hbuf = pair % 2
            b0, b1 = (0, 4 * c) if hist else (4 * c, 4 * c + 4)
            if b1 == b0:
                return
            dma("sp", KTh[hbuf][:, b0 * 128:b1 * 128], KTs[l].ap()[pair, :, b0 * 128:b1 * 128], "kld%d" % hbuf,
                [KTs_t[l][pair]], [KTh_t[hbuf]], chain=hist)
            dma("sp", Vh[hbuf][:, b0:b1, :],
                Vs[l].ap()[b0 * 128:b1 * 128, pair * 128:(pair + 1) * 128].rearrange("(b p) f -> p b f", p=128),
                "vld%d" % hbuf, Vs_t[l][pair // 4], [Vh_t[hbuf]], chain=hist)

        @staticmethod
        def v2(ap2, c0):
            if c0 == 0:
                return ap2
            return ap2.rearrange("p (h n) -> p h n", h=2)[:, :, c0:512]

        def col0(self, kb):
            di = kb - 4 * self.c
            return 128 * di if di > 0 else 0

        def s_z(self, i):
            c, l, nkb = self.c, self.l, self.nkb
            pair, kb = self.steps[i]
            hbuf = pair % 2
            if kb == nkb - 1:
                if pair >= 2:
                    self.kv_load(pair, True)
                self.kv_load(pair, False)
            k = ring["n0"]
            ring["n0"] += 1
            z0 = self.zring[k % len(self.zring)]
            idx = (k % NE, k % NSP, k % NEX, k % NA)
            ei, si, xi, ai = idx
            di = kb - 4 * c
            c0 = self.col0(kb)
            for hh in range(2):
                base = hh * 64
                zb = z0 + hh
                mm(zb, banks[zb][:, c0:512], KTh[hbuf][base:base + 64, kb * 128:(kb + 1) * 128],
                   self.QT[base:base + 64, pair, c0:512], True, di < 0, [KTh_t[hbuf], self.QT_t],
                   sig=(di < 0 and hh == 1))
                if di >= 0:
                    dc = 128 * di
                    mm(zb, banks[zb][:, dc:dc + 128], ident[:], negm[:], False, True, [cst_t], sig=(hh == 1))
            zz = psum_all[:, z0 * 512:(z0 + 2) * 512]
            act(self.v2(E2[ei][:], c0), self.v2(zz, c0), AF.Exp, [bank_t[z0], bank_t[z0 + 1]], [E2_t[ei]],
                scale=0.125)
            act(self.v2(SP2[si][:], c0), self.v2(E2[ei][:], c0), AF.Ln, [E2_t[ei]], [SP2_t[si]], bias=1.0)
            return idx

        def s_tri(self, i, idx):
            pair, kb = self.steps[i]
            ei, si, xi, ai = idx
            c0 = self.col0(kb)
            for hh in range(2):
                mm(hh, banks[hh][:, c0:512], tri[:], SP2[si][:, hh * 512 + c0:(hh + 1) * 512], kb == self.nkb - 1,
                   False, [cst_t, SP2_t[si]], sig=(hh == 1))
            act(self.v2(EX2[xi][:], c0), self.v2(psum_all[:, 0:1024], c0), AF.Exp, [bank_t[0], bank_t[1]],
                [EX2_t[xi]], scale=-1.0)

        def s_omt(self, i, idx):
            pair, kb = self.steps[i]
            ei, si, xi, ai = idx
            c0 = self.col0(kb)
            if kb > 0:
                for hh in range(2):
                    mm(hh, banks[hh][:, c0:512], omt[:], SP2[si][:, hh * 512 + c0:(hh + 1) * 512], False, False,
                       [cst_t, SP2_t[si]], sig=(hh == 1))
            tt("dve", self.v2(A2[ai][:], c0), self.v2(E2[ei][:], c0), self.v2(EX2[xi][:], c0), ALU.mult,
               [E2_t[ei], EX2_t[xi]], [A2_t[ai]])

        def s_av(self, i, idx):
            pair, kb = self.steps[i]
            ei, si, xi, ai = idx
            hbuf = pair % 2
            ob = self.obanks[pair % len(self.obanks)]
            c0 = self.col0(kb)
            for hh in range(2):
                base = hh * 64
                mm(ob, banks[ob][base:base + 64, c0:512], Vh[hbuf][:, kb, base:base + 64],
                   A2[ai][:, hh * 512 + c0:(hh + 1) * 512], kb == self.nkb - 1, kb == 0, [Vh_t[hbuf], A2_t[ai]],
                   sig=(hh == 1))
            if kb == 0:
                evac(self.OT[:, pair, :], banks[ob][:], [bank_t[ob]], [self.OT_t])

        def step(self, allow_new):
            it = self.it
            if allow_new and self.nxt < len(self.steps):
                i = self.nxt
                self.nxt += 1
                self.fly[i] = (it, self.s_z(i))
            for i in sorted(self.fly):
                t0, idx = self.fly[i]
                if it - t0 == 2:
                    self.s_omt(i, idx)
            for i in sorted(self.fly):
                t0, idx = self.fly[i]
                if it - t0 == 1:
                    self.s_tri(i, idx)
            for i in sorted(self.fly):
                t0, idx = self.fly[i]
                if it - t0 == 3:
                    self.s_av(i, idx)
                    del self.fly[i]
            self.it += 1

        def drain(self):
            while self.nxt < len(self.steps) and self.steps[self.nxt][1] != self.nkb - 1:
                self.step(True)
            while self.fly:
                self.step(False)

        def finish(self):
            while not self.done():
                self.step(True)

    CTX_Y = (xnT, xnT_t, QT, QT_t)
    CTX_X = (xnT_X, xnT_X_t, QT_X, QT_X_t)

    def shared_kv(c, hb):
        norm_T(hb, 8)
        wt = cast_weight("w_kv")
        a = wbf["w_kv"].ap()
        s = load_slab([(0, 64, a[:, 0:64]), (64, 64, a[:, 0:64]), (128, 64, a[:, 64:128]),
                       (192, 64, a[:, 64:128]), (256, 128, a[:, 128:256])], wt)
        if c > 0:
            for g in range(2):
                P.op("pool", lambda e, g=g: e.tensor_copy(KB[g][:, 0:128], KB[g][:, 512:640]),
                     reads=[KB_t[g]], writes=[KB_t[g]])
            P.op("pool", lambda e: e.tensor_copy(VB[:, 0, :, :], VB[:, 4, :, :]), reads=[VB_t], writes=[VB_t])
        for g in range(2):
            b = gbank()
            for kc in range(8):
                mm(b, banks[b][:], slab[s][:, kc, g * 128:(g + 1) * 128], xnT[:, kc, :], kc == 0, kc == 7,
                   [slab_t[s], xnT_t])
            act(KB[g][:, 128:640], banks[b][:], AF.Identity, [bank_t[b], cst_t], [KB_t[g]], bias=bk[:, g:g + 1])
        for tb in range(4):
            b = gbank()
            for kc in range(8):
                mm(b, banks[b][:, 0:128], xnT[:, kc, tb * 128:(tb + 1) * 128], slab[s][:, kc, 256:384],
                   kc == 0, kc == 7, [xnT_t, slab_t[s]])
            tt("dve", VB[:, tb + 1, :, 0:64], banks[b][:, 0:128].rearrange("p (g d) -> p g d", g=2),
               bv[:].rearrange("p (g d) -> p g d", g=2), ALU.add, [bank_t[b], cst_t], [VB_t])

    def layer_b(c, hb, j):
        norm_T(hb, 6 + j)
        for s2 in range(2):
            s = wslab("b_wq", j, 0, s2 * 512)
            for pp in range(4):
                pair = s2 * 4 + pp
                proj_featmajor(s, pp * 128, QT, QT_t, pair, bias=bq[:, j, pair:pair + 1])

        def parts_of(tb):
            return [1] + ([0] if 4 * c + tb > 0 else [])

        def stage_a(tb):
            PTb, PTb_t = PT[tb % len(PT)], PT_t[tb % len(PT)]
            for part in parts_of(tb):
                kc0 = (tb + part) * 128
                for hg in range(4):
                    g8, par = hg // 2, hg % 2
                    base = par * 64
                    b = 5 + par
                    b2 = 3 + par
                    for hi in range(4):
                        hd = g8 * 8 + par + 2 * hi
                        mm(b, banks[b][:, hi * 128:(hi + 1) * 128], KB[g8][base:base + 64, kc0:kc0 + 128],
                           QT[base:base + 64, hd // 2, tb * 128:(tb + 1) * 128], True, True,
                           [KB_t[g8], QT_t], sig=(hi == 3))
                    h0 = g8 * 8 + par
                    act(banks[b2][:], banks[b][:], AF.Exp, [bank_t[b]], [bank_t[b2]], scale=0.125)
                    tt("dve", PTb[:, part, h0:h0 + 7:2, :], banks[b2][:].rearrange("p (a t) -> p a t", a=4),
                       EB[:, part, h0:h0 + 7:2, :], ALU.mult, [bank_t[b2], EB_t], [PTb_t])

        def stage_b(tb):
            PTb, PTb_t = PT[tb % len(PT)], PT_t[tb % len(PT)]
            parts = parts_of(tb)
            for hd in range(16):
                g = hd // 8
                ob = hd // 7
                oc = (hd % 7) * 66
                for pi, part in enumerate(parts):
                    mm(ob, banks[ob][:, oc:oc + 66], PTb[:, part, hd, :], VB[:, tb + part, g, 0:66],
                       pi == 0, pi == len(parts) - 1, [PTb_t, VB_t],
                       sig=(pi == len(parts) - 1 and (hd % 7 == 6 or hd == 15)))
            for ob in range(3):
                nh = 7 if ob < 2 else 2
                bv3 = banks[ob][:, 0:nh * 66].rearrange("p (a d) -> p a d", d=66)
                tt("dve", den[:, 7 * ob:7 * ob + nh], bv3[:, :, 64], ES[:, j * 16 + 7 * ob:j * 16 + 7 * ob + nh],
                   ALU.add, [bank_t[ob], cst_t], [den_t])
            P.op("dve", lambda e: e.reciprocal(den[:, 16:32], den[:, 0:16]), reads=[den_t], writes=[den_t])
            for ob in range(3):
                nh = 7 if ob < 2 else 2
                bv3 = banks[ob][:, 0:nh * 66].rearrange("p (a d) -> p a d", d=66)
                rv = bass.AP(den, 16 + 7 * ob, [[32, 128], [1, nh], [0, 64]])
                tt("dve", Otok[:, 7 * ob * 64:(7 * ob + nh) * 64].rearrange("p (a d) -> p a d", d=64),
                   bv3[:, :, 0:64], rv, ALU.mult, [bank_t[ob], den_t], [Otok_t])
            tpv = bf_bank(7)
            for kc in range(8):
                P.op("pe", lambda e, kc=kc: e.transpose(tpv[:, kc * 128:(kc + 1) * 128],
                                                        Otok[:, kc * 128:(kc + 1) * 128], ident[:]),
                     reads=[Otok_t, cst_t], writes=[bank_t[7]], sig=(kc == 7))
            evac(OT[:, :, tb * 128:(tb + 1) * 128], tpv.rearrange("p (k t) -> p k t", k=8), [bank_t[7]], [OT_t])

        for tb in range(4):
            stage_a(tb)
            stage_b(tb)
        dma("pool", bo[:], bo_d.ap()[0][j * D:(j + 1) * D].partition_broadcast(128), "bo", [], [bo_t])
        for tb in range(4):
            P.op("pool", lambda e, tb=tb: e.tensor_tensor(h[hb][:, tb, :], h[hb][:, tb, :], bo[:], ALU.add),
                 reads=[h_t[hb], bo_t], writes=[h_t[hb]])
        resid_proj(hb, OT, OT_t, "b_wo", j)

    def final_out(c, hb, do_norm=True):
        for tb in range(4):
            stat_t = stat_ts[tb]
            if do_norm:
                P.op("dve", lambda e, tb=tb: e.scalar_tensor_tensor(junk[:], h[hb][:, tb, :], 1.0, h[hb][:, tb, :],
                                                                    ALU.mult, ALU.mult,
                                                                    accum_out=stat[:, tb:tb + 1]),
                     reads=[h_t[hb]], writes=[junk_t, stat_t])
                act(stat[:, 4 + tb:5 + tb], stat[:, tb:tb + 1], AF.Ln, [stat_t], [stat_t], scale=1.0 / D, bias=EPS)
                act(stat[:, 8 + tb:9 + tb], stat[:, 4 + tb:5 + tb], AF.Exp, [stat_t], [stat_t], scale=-0.5)
                P.op("dve", lambda e, tb=tb: e.scalar_tensor_tensor(h[hb][:, tb, :], h[hb][:, tb, :],
                                                                    stat[:, 8 + tb:9 + tb], gF[:],
                                                                    ALU.mult, ALU.mult),
                     reads=[h_t[hb], stat_t, cst_t], writes=[h_t[hb]])
                dma("pool", out_d.ap()[c * TILE + tb * 128:c * TILE + (tb + 1) * 128, :], h[hb][:, tb, :], "out",
                    [h_t[hb]], [])
            else:
                dma("pool", out_d.ap()[c * TILE + tb * 128:c * TILE + (tb + 1) * 128, :], h[hb][:, tb, :], "out",
                    [h_t[hb]], [])

    setup_consts()
    dma("pool", h[0][:], x_d.ap()[0:TILE, :].rearrange("(tb p) f -> p tb f", p=128), "xld", [], [h_t[0]])
    P.op("pool", lambda e: e.memset(VB[:], 0.0), writes=[VB_t])
    P.op("pool", lambda e: e.memset(VB[:, :, :, 64:65], 1.0), writes=[VB_t])

    def x_start(c):
        a_front(c, 0, 0, CTX_X)
        return Attn(c, 0, QT_X, QT_X_t, OT_X, OT_X_t, zring=(3,), obanks=(2,))

    def cast_all():
        for l in range(n_a):
            for nm in ("a_wqkv", "a_wo", "mlp_up", "mlp_down"):
                cast_weight(nm, l)
        for j in range(n_b):
            if j == 0:
                cast_weight("w_kv")
            cast_weight("b_wq", j)
            cast_weight("b_wo", j)
            cast_weight("mlp_up", 2 + j)
            cast_weight("mlp_down", 2 + j)

    n_mlp = n_a + n_b
    if n_a > 0:
        cast_weight("a_wqkv", 0)
        g0 = x_start(0)
        cast_all()
        g0.finish()
    else:
        cast_all()
    for c in range(n_tiles):
        hb = 0
        if c > 0:
            dma("pool", h[0][:], x_d.ap()[c * TILE:(c + 1) * TILE, :].rearrange("(tb p) f -> p tb f", p=128),
                "xld", [], [h_t[0]])
        xstate["g"] = None
        if n_a > 0:
            resid_proj(hb, OT_X, OT_X_t, "a_wo", 0)
            if c + 1 < n_tiles:
                g = x_start(c + 1)
                xstate.update(g=g, acc=0.0, ratio=g.n_iters() / (0.9 * 64 * n_mlp))
            mlp(hb, 0)
            for l in range(1, n_a):
                a_front(c, l, hb, CTX_Y)
                Attn(c, l, QT, QT_t, OT, OT_t, zring=(3, 5), obanks=(2, 7)).finish()
                resid_proj(hb, OT, OT_t, "a_wo", l)
                mlp(hb, l)
        for j in range(n_b):
            if j == 0:
                shared_kv(c, hb)
            layer_b(c, hb, j)
            mlp(hb, 2 + j)
        if xstate["g"] is not None:
            xstate["g"].finish()
        final_out(c, hb, do_norm=not dbg)

    if dbg:
        print("sbuf bytes remaining", nc.sbuf_bytes_remaining)
    sem_names = ["pe", "act", "dve", "pool"]
    sems = {n: es.enter_context(nc.semaphore("sem_" + n)) for n in sem_names}
    dsems = {n: es.enter_context(nc.semaphore("dsem_" + n)) for n in P.dma_val}
    with nc.Block() as block:
        P.emit(nc, block, sems, dsems)
    es.close()
    return nc


def _bucket_onehot():
    n = np.arange(128)
    nf = np.maximum(n, 1).astype(np.float32)
    large = 16 + (np.log(nf / np.float32(16)) / np.float32(np.log(128 / 16)) * np.float32(16)).astype(np.int32)
    large = np.minimum(large, 31)
    bucket = np.where(n < 16, n, large)
    oh = np.zeros((32, 128), np.float32)
    oh[bucket, n] = 1.0
    return oh


def make_in_maps(inp, n_cores, S):
    f = lambda a: np.ascontiguousarray(np.asarray(a, dtype=np.float32))
    g_all = np.stack([inp["a_norm"][0], inp["a_norm"][1], inp["mlp_norm"][0], inp["mlp_norm"][1],
                      inp["mlp_norm"][2], inp["mlp_norm"][3], inp["b_norm"][0], inp["b_norm"][1],
                      inp["kv_norm"]], 0)
    gT = f(np.asarray(g_all).reshape(9, 8, 128).transpose(2, 0, 1))
    bqT = f(np.asarray(inp["b_bq"]).reshape(2, 8, 128).transpose(2, 0, 1))
    bkv = np.asarray(inp["b_kv"])
    bk_dup = f(np.stack([np.concatenate([bkv[0:64], bkv[0:64]]), np.concatenate([bkv[64:128], bkv[64:128]])], 1))
    shared = {
        "a_wqkv": f(inp["a_wqkv"]), "a_wo": f(inp["a_wo"]), "w_kv": f(inp["w_kv"]), "b_wq": f(inp["b_wq"]),
        "b_wo": f(inp["b_wo"]), "mlp_up": f(inp["mlp_up"]), "mlp_down": f(inp["mlp_down"]),
        "gT": gT, "gF": f(np.asarray(inp["final_norm"]).reshape(1, D)), "bqT": bqT, "bk_dup": bk_dup,
        "bv": f(bkv[128:256].reshape(1, 128)), "bo": f(np.asarray(inp["b_bo"]).reshape(1, 2 * D)),
        "sinks": f(np.asarray(inp["b_sinks"]).reshape(1, 32)), "rel_bias": f(inp["rel_bias"]),
        "onehot": _bucket_onehot(),
    }
    x = np.asarray(inp["x"], dtype=np.float32)
    maps = []
    for b in range(n_cores):
        m = dict(shared)
        m["x"] = np.ascontiguousarray(x[b, :S])
        maps.append(m)
    return maps


def kernel(**inputs):
    nc = bass.Bass("TRN2", target_bir_lowering=False)
    build(nc)
    in_maps = make_in_maps(inputs, 8, SEQ)
    res = run_bass_kernel_spmd(nc, in_maps, core_ids=list(range(8)))
    return np.stack([np.asarray(r["out"]) for r in res.results], 0).astype(np.float32)
```

```python
import numpy as np
from contextlib import ExitStack
import concourse.bass as bass
import concourse.mybir as mybir
from concourse.bass_utils import run_bass_kernel_spmd

F32, BF16 = mybir.dt.float32, mybir.dt.bfloat16
AF = mybir.ActivationFunctionType
ALU = mybir.AluOpType

D = 1024
SEQ = 4096
TILE = 512
DFF = 4096
NH = 16
EPS = 1e-5
NEG = -30000.0
NSLAB = 3


class T:
    __slots__ = ("name", "w", "r")

    def __init__(self, name):
        self.name = name
        self.w = None
        self.r = {}


class Prog:
    ENGS = ("pe", "act", "dve", "pool", "sp")

    def __init__(self):
        self.ops = {e: [] for e in self.ENGS}
        self.seen = {e: {} for e in self.ENGS}
        self.dma_val = {}

    def op(self, eng, fn, reads=(), writes=(), sig=True, dma=None, chain=True):
        deps = {}
        self.seq = getattr(self, "seq", 0) + 1

        def add(tok):
            if tok is None:
                return
            key = tok[:2]
            if deps.get(key, -1) < tok[2]:
                deps[key] = tok[2]

        for t in reads:
            add(t.w)
        for t in writes:
            if chain:
                add(t.w)
            for tok in t.r.values():
                add(tok)
        waits = []
        seen = self.seen[eng]
        for key, val in deps.items():
            if key[0] == "e" and key[1] == eng and eng in ("pe",):
                continue
            if seen.get(key, -1) >= val:
                continue
            seen[key] = val
            waits.append((key[0], key[1], val))
        idx = len(self.ops[eng])
        if dma is not None:
            v = self.dma_val.get(dma, 0) + 16
            self.dma_val[dma] = v
            tok = ("d", dma, v)
        else:
            tok = ("e", eng, idx)
        self.ops[eng].append((waits, fn, sig and dma is None, dma, self.seq))
        for t in reads:
            t.r[tok[:2]] = tok
        for t in writes:
            t.w = tok
            t.r = {}
        return tok

    def emit(self, nc, block, sems, dsems):
        cnt = {}
        for e in self.ENGS:
            c = 0
            arr = []
            for (_, _, sig, _, sq) in self.ops[e]:
                if sig:
                    c += 1
                arr.append((c, sig, sq))
            res = [None] * len(arr)
            nxt = None
            for i in range(len(arr) - 1, -1, -1):
                if arr[i][1]:
                    nxt = (arr[i][0], arr[i][2])
                res[i] = nxt
            cnt[e] = res
        regs = {"pe": block.tensor, "act": block.scalar, "dve": block.vector,
                "pool": block.gpsimd, "sp": block.sync}
        final = dict(self.dma_val)

        def make(e):
            def body(eng):
                for (waits, fn, sig, dma, sq) in self.ops[e]:
                    for (k, name, val) in waits:
                        if k == "e":
                            v = cnt[name][val]
                            assert v is not None, (e, name, val)
                            assert v[1] < sq, ("deadlock risk", e, name, val, v, sq)
                            eng.wait_ge(sems[name], v[0])
                        else:
                            eng.wait_ge(dsems[name], val)
                    ins = fn(eng)
                    if dma is not None:
                        ins.then_inc(dsems[dma], 16)
                    elif sig:
                        ins.then_inc(sems[e], 1)
                if e == "pool":
                    for name, val in final.items():
                        eng.wait_ge(dsems[name], val)
            return body

        for e in self.ENGS:
            regs[e](make(e))


def build(nc, n_tiles=8, n_a=2, n_b=2, dbg=False):
    S = n_tiles * TILE
    P = Prog()
    es = ExitStack()

    def dram(name, shape, dt, kind):
        return nc.dram_tensor(name, list(shape), dt, kind=kind)

    x_d = dram("x", [S, D], F32, "ExternalInput")
    out_d = dram("out", [S, D], F32, "ExternalOutput")
    wsrc = {
        "a_wqkv": dram("a_wqkv", [2, D, 3 * D], F32, "ExternalInput"),
        "a_wo": dram("a_wo", [2, D, D], F32, "ExternalInput"),
        "w_kv": dram("w_kv", [D, 256], F32, "ExternalInput"),
        "b_wq": dram("b_wq", [2, D, D], F32, "ExternalInput"),
        "b_wo": dram("b_wo", [2, D, D], F32, "ExternalInput"),
        "mlp_up": dram("mlp_up", [4, D, DFF], F32, "ExternalInput"),
        "mlp_down": dram("mlp_down", [4, DFF, D], F32, "ExternalInput"),
    }
    gT_d = dram("gT", [128, 9, 8], F32, "ExternalInput")
    gF_d = dram("gF", [1, D], F32, "ExternalInput")
    bq_d = dram("bqT", [128, 2, 8], F32, "ExternalInput")
    bk_d = dram("bk_dup", [128, 2], F32, "ExternalInput")
    bv_d = dram("bv", [1, 128], F32, "ExternalInput")
    bo_d = dram("bo", [1, 2 * D], F32, "ExternalInput")
    sk_d = dram("sinks", [1, 32], F32, "ExternalInput")
    rb_d = dram("rel_bias", [32, 16], F32, "ExternalInput")
    oh_d = dram("onehot", [32, 128], F32, "ExternalInput")

    def slab_shape(v):
        sh = list(v.shape)
        if len(sh) == 2:
            return sh
        return [sh[0], sh[1] // 1024, sh[2] // 512, 128, 8, 512]

    wbf = {k: dram(k + "_bf", slab_shape(v), BF16, "Internal") for k, v in wsrc.items()}
    wbf_t = {}
    KTs = [dram("KTs%d" % l, [8, 128, S], BF16, "Internal") for l in range(2)]
    Vs = [dram("Vs%d" % l, [S, D], BF16, "Internal") for l in range(2)]
    KTs_t = [[T("KTs%d_%d" % (l, p)) for p in range(8)] for l in range(2)]
    Vs_t = [[[T("Vs%d_%d_%d" % (l, hf, tb)) for tb in range(4)] for hf in range(2)] for l in range(2)]
    G_d = dram("Gscr", [16, 128, 384], F32, "Internal")
    G_t = T("Gscr")

    def sb(name, shape, dt):
        return es.enter_context(nc.sbuf_tensor(name, list(shape), dt))

    psum_all = es.enter_context(nc.psum_tensor("psum_all", [128, 4096], F32))
    banks = [psum_all[:, i * 512:(i + 1) * 512] for i in range(8)]
    bank_t = [T("bank%d" % i) for i in range(8)]

    h = [sb("h%d" % i, [128, 4, D], F32) for i in range(1)]
    h_t = [T("h0")]
    xhat = [sb("xhat%d" % i, [128, D], BF16) for i in range(2)]
    xhat_t = [T("xhat0"), T("xhat1")]
    stat = sb("stat", [128, 32], F32)
    stat_ts = [T("stat%d" % i) for i in range(4)]
    stat_tsX = [T("statX%d" % i) for i in range(4)]
    xnT = sb("xnT", [128, 8, TILE], BF16)
    xnT_t = T("xnT")
    big = [sb("big%d" % i, [128, 8, TILE], BF16) for i in range(2)]
    big_t = [T("big%d" % i) for i in range(2)]
    QT, OT = big
    QT_t, OT_t = big_t
    kvs = [sb("kvs%d" % i, [128, 512], BF16) for i in range(4)]
    kvs_t = [T("kvs%d" % i) for i in range(4)]
    relu_b = [sb("relu%d" % i, [128, 512], F32) for i in range(2)]
    relu_t = [T("relu0"), T("relu1")]
    xs = sb("xs", [128, D], F32)
    xs_t = T("xs")
    xnT_X = sb("xnT_X", [128, 8, TILE], BF16)
    xnT_X_t = T("xnT_X")
    QT_X = sb("QT_X", [128, 8, TILE], BF16)
    QT_X_t = T("QT_X")
    OT_X = sb("OT_X", [128, 8, TILE], BF16)
    OT_X_t = T("OT_X")
    slab = [sb("slab%d" % i, [128, 8, 512], BF16) for i in range(NSLAB)]
    slab_t = [T("slab%d" % i) for i in range(NSLAB)]
    NE, NSP, NEX, NA = 3, 3, 2, 2
    E2 = [sb("E2_%d" % i, [128, 1024], F32) for i in range(NE)]
    E2_t = [T("E2_%d" % i) for i in range(NE)]
    SP2 = [sb("SP2_%d" % i, [128, 1024], BF16) for i in range(NSP)]
    SP2_t = [T("SP2_%d" % i) for i in range(NSP)]
    EX2 = [sb("EX2_%d" % i, [128, 1024], F32) for i in range(NEX)]
    EX2_t = [T("EX2_%d" % i) for i in range(NEX)]
    A2 = [sb("A2_%d" % i, [128, 1024], BF16) for i in range(NA)]
    A2_t = [T("A2_%d" % i) for i in range(NA)]
    Ebuf = [E2[0][:, 0:512]]
    E_t = [E2_t[0]]
    KTh = [sb("KTh%d" % i, [128, S], BF16) for i in range(2)]
    KTh_t = [T("KTh0"), T("KTh1")]
    Vh = [sb("Vh%d" % i, [128, S // 128, 128], BF16) for i in range(2)]
    Vh_t = [T("Vh0"), T("Vh1")]
    Rb, Rb_t = Ebuf[0], E_t[0]
    ident = sb("ident", [128, 128], BF16)
    tri = sb("tri", [128, 128], BF16)
    omt = sb("omt", [128, 128], BF16)
    negm = sb("negm", [128, 128], BF16)
    cst_t = T("consts")
    gT = sb("gTs", [128, 9, 8], F32)
    gF = sb("gFs", [128, D], F32)
    bq = sb("bqs", [128, 2, 8], F32)
    bk = sb("bks", [128, 2], F32)
    bv = sb("bvs", [128, 128], F32)
    bo = sb("bos", [128, D], F32)
    bo_t = T("bo")
    ES = sb("ESs", [128, 32], F32)
    EB = sb("EBs", [128, 2, 16, 128], F32)
    EB_t = T("EB")
    KB = [sb("KB%d" % g, [128, 640], BF16) for g in range(2)]
    KB_t = [T("KB0"), T("KB1")]
    VB = sb("VB", [128, 5, 2, 80], BF16)
    VB_t = T("VB")
    PT = [sb("PT%d" % i, [128, 2, 16, 128], BF16) for i in range(1)]
    PT_t = [T("PT0")]
    Otok = sb("Otok", [128, D], BF16)
    Otok_t = T("Otok")
    junk, junk_t = Otok, Otok_t
    den = sb("den", [128, 32], F32)
    den_t = T("den")

    def bf_bank(i):
        return banks[i][:].bitcast(BF16)

    def mm(bank, out, lhsT, rhs, start, stop, reads, sig=None):
        P.op("pe", lambda e: e.matmul(out, lhsT, rhs, start=start, stop=stop, skip_group_check=True),
             reads=reads, writes=[bank_t[bank]], sig=(stop if sig is None else sig))

    def act(out, in_, func, reads, writes, scale=1.0, bias=0.0, accum=None):
        def fn(e):
            kw = {}
            if accum is not None:
                kw["accum_out"] = accum
            return e.activation(out, in_, func, bias=bias, scale=scale, **kw)
        P.op("act", fn, reads=reads, writes=writes)

    def tt(eng, out, in0, in1, op, reads, writes):
        P.op(eng, lambda e: e.tensor_tensor(out, in0, in1, op), reads=reads, writes=writes)

    def dma(eng, out, in_, sem, reads, writes, chain=True, **kw):
        P.op(eng, lambda e: e.dma_start(out, in_, **kw), reads=reads, writes=writes, dma=sem, chain=chain)

    def setup_consts():
        w = [cst_t]
        P.op("pool", lambda e: e.memset(junk[:, 0:512], 1.0), writes=[junk_t])
        P.op("pool", lambda e: e.memset(Rb[:], 0.0), writes=[Rb_t])
        P.op("pool", lambda e: e.affine_select(ident[:], junk[:, 0:128], [[-1, 128]], ALU.is_equal, 0.0,
                                               base=0, channel_multiplier=1), reads=[junk_t], writes=w)
        P.op("pool", lambda e: e.affine_select(tri[:], junk[:, 0:128], [[-1, 128]], ALU.is_ge, 0.0,
                                               base=0, channel_multiplier=1), reads=[junk_t], writes=w)
        P.op("pool", lambda e: e.affine_select(omt[:], junk[:, 0:128], [[1, 128]], ALU.is_gt, 0.0,
                                               base=0, channel_multiplier=-1), reads=[junk_t], writes=w)
        P.op("pool", lambda e: e.affine_select(negm[:], Rb[:].bitcast(BF16)[:, 0:128], [[1, 128]],
                                               ALU.is_gt, NEG, base=0, channel_multiplier=-1),
             reads=[Rb_t], writes=w)
        dma("sp", gT[:], gT_d.ap(), "cst", [], w)
        dma("sp", bq[:], bq_d.ap(), "cst", [], w)
        dma("sp", bk[:], bk_d.ap(), "cst", [], w)
        dma("sp", gF[:], gF_d.ap()[0].partition_broadcast(128), "cst", [], w)
        dma("sp", bv[:], bv_d.ap()[0].partition_broadcast(128), "cst", [], w)
        dma("sp", ES[:], sk_d.ap()[0].partition_broadcast(128), "cst", [], w)
        if n_b > 0:
            act(ES[:], ES[:], AF.Exp, [cst_t], w)
            dma("sp", Ebuf[0][0:32, 0:16], rb_d.ap(), "cst2", [], [E_t[0]])
            dma("sp", Ebuf[0][0:32, 128:256], oh_d.ap(), "cst2", [], [E_t[0]])
            P.op("pe", lambda e: e.matmul(banks[5][0:16, 0:128], Ebuf[0][0:32, 0:16], Ebuf[0][0:32, 128:256],
                                          start=True, stop=True), reads=[E_t[0]], writes=[bank_t[5]])
            P.op("dve", lambda e: e.memset(E2[1][0:16, 0:384], 0.0), writes=[E2_t[1]])
            act(E2[1][0:16, 128:256], banks[5][0:16, 0:128], AF.Exp, [bank_t[5]], [E2_t[1]])
            dma("pool", G_d.ap(), bass.AP(E2[1], 0, [[1024, 16], [0, 128], [1, 384]]), "cst3", [E2_t[1]], [G_t])
            for part in range(2):
                off = 128 + 128 * (1 - part)
                src = bass.AP(G_d, off, [[383, 128], [128 * 384, 16], [1, 128]])
                dma("pool", EB[:, part, :, :], src, "cst4", [G_t], [EB_t])

    def cast_weight(name, layer=None):
        key = (name, layer)
        if key in wbf_t:
            return wbf_t[key]
        t = T("wbf_%s_%s" % (name, layer))
        wbf_t[key] = t
        src = wsrc[name].ap()
        dst = wbf[name].ap()
        sem = "cast_%s_%s" % (name, layer)
        if layer is None:
            R, C = src.shape
            for r0 in range(0, R, 512):
                dma("pool", dst[r0:r0 + 512, :], src[r0:r0 + 512, :], sem, [], [t], chain=False)
        else:
            src = src[layer]
            R, C = src.shape
            for kb in range(R // 1024):
                for sc in range(C // 512):
                    dma("pool", dst[layer, kb, sc],
                        src[kb * 1024:(kb + 1) * 1024, sc * 512:(sc + 1) * 512].rearrange("(kc p) n -> p kc n", p=128),
                        sem, [], [t], chain=False)
        return t

    slab_state = {"n": 0}

    def load_slab(pieces, wt):
        i = slab_state["n"]
        slab_state["n"] += 1
        b = i % NSLAB
        for (c0, ncol, src) in pieces:
            dma("sp", slab[b][:, :, c0:c0 + ncol], src.rearrange("(kc p) n -> p kc n", p=128),
                "slab%d" % b, [wt], [slab_t[b]])
        return b

    def wslab(name, layer, r0, c0):
        wt = cast_weight(name, layer)
        i = slab_state["n"]
        slab_state["n"] += 1
        b = i % NSLAB
        dma("sp", slab[b][:], wbf[name].ap()[layer, r0 // 1024, c0 // 512], "slab%d" % b, [wt], [slab_t[b]])
        return b

    rot = {"g": 0}
    ring = {"n0": 0}

    def gbank(pool=(5, 6, 7)):
        b = pool[rot["g"] % len(pool)]
        rot["g"] += 1
        return b

    cp = {"n": 0}

    def evac(out, in_, reads, writes, eng=None):
        if eng is None:
            eng = "dve" if cp["n"] % 2 == 0 else "act"
            cp["n"] += 1
        if eng == "dve":
            P.op("dve", lambda e: e.tensor_copy(out, in_), reads=reads, writes=writes)
        else:
            act(out, in_, AF.Copy, reads, writes)

    def norm_T(hb, gi, dst=None, dst_t=None, xsrc=None):
        if dst is None:
            dst, dst_t = xnT, xnT_t
        gv = bass.AP(gT, gi * 8, [[72, 128], [1, 8], [0, 128]])
        so = 0 if xsrc is None else 16
        sts = stat_ts if xsrc is None else stat_tsX

        def src(tb):
            return (h[hb][:, tb, :], h_t[hb]) if xsrc is None else (xs[:], xs_t)

        def ss(tb):
            a, at = src(tb)
            P.op("dve", lambda e: e.scalar_tensor_tensor(junk[:], a, 1.0, a, ALU.mult, ALU.mult,
                                                         accum_out=stat[:, so + tb:so + tb + 1]),
                 reads=[at], writes=[junk_t, sts[tb]])

        def rs(tb):
            act(stat[:, so + 4 + tb:so + 5 + tb], stat[:, so + tb:so + tb + 1], AF.Ln, [sts[tb]], [sts[tb]],
                scale=1.0 / D, bias=EPS)
            act(stat[:, so + 8 + tb:so + 9 + tb], stat[:, so + 4 + tb:so + 5 + tb], AF.Exp, [sts[tb]], [sts[tb]],
                scale=-0.5)

        def ts(tb):
            a, at = src(tb)
            xb = tb % 2
            P.op("dve", lambda e: e.tensor_scalar(xhat[xb][:], a, stat[:, so + 8 + tb:so + 9 + tb], None, ALU.mult),
                 reads=[at, sts[tb]], writes=[xhat_t[xb]])

        def tr(tb):
            xb = tb % 2
            tpb = (7, 6)[tb % 2]
            tpv = bf_bank(tpb)
            for kc in range(8):
                P.op("pe", lambda e, kc=kc: e.transpose(tpv[:, kc * 128:(kc + 1) * 128],
                                                        xhat[xb][:, kc * 128:(kc + 1) * 128], ident[:]),
                     reads=[xhat_t[xb], cst_t], writes=[bank_t[tpb]], sig=(kc == 7))

        def ev(tb):
            tpb = (7, 6)[tb % 2]
            tpv = bf_bank(tpb)
            tt("dve", dst[:, :, tb * 128:(tb + 1) * 128], tpv.rearrange("p (k t) -> p k t", k=8), gv, ALU.mult,
               [bank_t[tpb], cst_t], [dst_t])

        if xsrc is None:
            for tb in range(4):
                ss(tb)
            for tb in range(4):
                rs(tb)
            ts(0); tr(0); ts(1); tr(1); ev(0); ts(2); tr(2); ev(1); ts(3); tr(3); ev(2); ev(3)
        else:
            for tb in range(4):
                r0 = xsrc * TILE + tb * 128
                dma("pool", xs[:], x_d.ap()[r0:r0 + 128, :], "xs", [], [xs_t])
                ss(tb); rs(tb); ts(tb); tr(tb); ev(tb)

    def proj_featmajor(sb_idx, col0, dst, dst_t, dst_chunk, bias=None, src=None, src_t=None):
        if src is None:
            src, src_t = xnT, xnT_t
        b = gbank()
        for kc in range(8):
            mm(b, banks[b][:], slab[sb_idx][:, kc, col0:col0 + 128], src[:, kc, :], kc == 0, kc == 7,
               [slab_t[sb_idx], src_t])
        out = dst[:, dst_chunk, :] if dst_chunk is not None else dst
        if bias is None:
            evac(out, banks[b][:], [bank_t[b]], [dst_t])
        else:
            act(out, banks[b][:], AF.Identity, [bank_t[b], cst_t], [dst_t], bias=bias)

    def resid_proj(hb, src, src_t, name, layer):
        for half in range(2):
            s = wslab(name, layer, 0, half * 512)
            for tb in range(4):
                b = gbank()
                for kc in range(8):
                    mm(b, banks[b][:], src[:, kc, tb * 128:(tb + 1) * 128], slab[s][:, kc, :], kc == 0, kc == 7,
                       [src_t, slab_t[s]])
                hv = h[hb][:, tb, half * 512:(half + 1) * 512]
                tt("dve", hv, banks[b][:], hv, ALU.add, [bank_t[b], h_t[hb]], [h_t[hb]])

    xstate = {"g": None, "acc": 0.0, "ratio": 0.0}

    def tick():
        g = xstate["g"]
        if g is None or g.done():
            return
        xstate["acc"] += xstate["ratio"]
        while xstate["acc"] >= 1.0 and not g.done():
            g.step(True)
            xstate["acc"] -= 1.0

    def mlp(hb, L):
        norm_T(hb, 2 + L)
        DB = (5, 6)
        for fh in range(2):
            for s4 in range(4):
                s = wslab("mlp_up", L, 0, (fh * 4 + s4) * 512)
                for jj in range(4):
                    jl = s4 * 4 + jj
                    b = gbank()
                    for kc in range(8):
                        mm(b, banks[b][:], slab[s][:, kc, jj * 128:(jj + 1) * 128], xnT[:, kc, :], kc == 0, kc == 7,
                           [slab_t[s], xnT_t])
                    rb = jl % 2
                    act(relu_b[rb][:], banks[b][:], AF.Relu, [bank_t[b]], [relu_t[rb]])
                    tt("dve", big[jl // 8][:, jl % 8, :], banks[b][:], relu_b[rb][:], ALU.mult,
                       [bank_t[b], relu_t[rb]], [big_t[jl // 8]])
                    tick()
            for half in range(2):
                for tbp in range(2):
                    for jg in range(2):
                        s = wslab("mlp_down", L, (fh * 16 + jg * 8) * 128, half * 512)
                        for t2 in range(2):
                            tb = tbp * 2 + t2
                            b = DB[t2]
                            for j in range(8):
                                mm(b, banks[b][:], big[jg][:, j, tb * 128:(tb + 1) * 128], slab[s][:, j, :],
                                   jg == 0 and j == 0, jg == 1 and j == 7, [big_t[jg], slab_t[s]])
                            tick()
                    for t2 in range(2):
                        tb = tbp * 2 + t2
                        hv = h[hb][:, tb, half * 512:(half + 1) * 512]
                        tt("dve", hv, banks[DB[t2]][:], hv, ALU.add, [bank_t[DB[t2]], h_t[hb]], [h_t[hb]])
        g = xstate["g"]
        if g is not None:
            g.drain()

    kvr = {"n": 0}

    def a_front(c, l, hb, ctx):
        xn_c, xn_ct, QT_c, QT_ct = ctx
        if ctx is CTX_X:
            norm_T(hb, l, dst=xn_c, dst_t=xn_ct, xsrc=c)
        else:
            norm_T(hb, l, dst=xn_c, dst_t=xn_ct)
        for s6 in range(4):
            s = wslab("a_wqkv", l, 0, s6 * 512)
            for pp in range(4):
                pair = (s6 % 2) * 4 + pp
                if s6 < 2:
                    proj_featmajor(s, pp * 128, QT_c, QT_ct, pair, src=xn_c, src_t=xn_ct)
                else:
                    r = kvr["n"] % 4
                    kvr["n"] += 1
                    proj_featmajor(s, pp * 128, kvs[r][:], kvs_t[r], None, src=xn_c, src_t=xn_ct)
                    dma("sp", KTs[l].ap()[pair, :, c * TILE:(c + 1) * TILE], kvs[r][:], "kvs%d" % r,
                        [kvs_t[r]], [KTs_t[l][pair]])
        for half in range(2):
            s = wslab("a_wqkv", l, 0, 2048 + half * 512)
            for tb in range(4):
                b = gbank()
                for kc in range(8):
                    mm(b, banks[b][:], xn_c[:, kc, tb * 128:(tb + 1) * 128], slab[s][:, kc, :], kc == 0, kc == 7,
                       [xn_ct, slab_t[s]])
                r = kvr["n"] % 4
                kvr["n"] += 1
                evac(kvs[r][:], banks[b][:], [bank_t[b]], [kvs_t[r]])
                r0 = c * TILE + tb * 128
                dma("sp", Vs[l].ap()[r0:r0 + 128, half * 512:(half + 1) * 512], kvs[r][:], "kvs%d" % r,
                    [kvs_t[r]], [Vs_t[l][half][tb]])

    class Attn:
        def __init__(self, c, l, QT_c, QT_ct, OT_c, OT_ct, zring, obanks):
            self.c, self.l = c, l
            self.QT, self.QT_t, self.OT, self.OT_t = QT_c, QT_ct, OT_c, OT_ct
            self.zring, self.obanks = zring, obanks
            self.nkb = 4 * c + 4
            self.steps = [(pair, kb) for pair in range(8) for kb in range(self.nkb - 1, -1, -1)]
            self.nxt = 0
            self.it = 0
            self.fly = {}
            self.kv_load(0, True)
            self.kv_load(1, True)

        def n_iters(self):
            return len(self.steps) + 3

        def done(self):
            return self.nxt >= len(self.steps) and not self.fly

        def kv_load(self, pair, hist):
            c, l = self.c, self.l
            hbuf = pair % 2
            b0, b1 = (0, 4 * c) if hist else (4 * c, 4 * c + 4)
            if b1 == b0:
                return
            dma("sp", KTh[hbuf][:, b0 * 128:b1 * 128], KTs[l].ap()[pair, :, b0 * 128:b1 * 128], "kld%d" % hbuf,
                [KTs_t[l][pair]], [KTh_t[hbuf]], chain=hist)
            dma("sp", Vh[hbuf][:, b0:b1, :],
                Vs[l].ap()[b0 * 128:b1 * 128, pair * 128:(pair + 1) * 128].rearrange("(b p) f -> p b f", p=128),
                "vld%d" % hbuf, Vs_t[l][pair // 4], [Vh_t[hbuf]], chain=hist)

        @staticmethod
        def v2(ap2, c0):
            if c0 == 0:
                return ap2
            return ap2.rearrange("p (h n) -> p h n", h=2)[:, :, c0:512]

        def col0(self, kb):
            di = kb - 4 * self.c
            return 128 * di if di > 0 else 0

        def s_z(self, i):
            c, l, nkb = self.c, self.l, self.nkb
            pair, kb = self.steps[i]
            hbuf = pair % 2
            if kb == nkb - 1:
                if pair >= 2:
                    self.kv_load(pair, True)
                self.kv_load(pair, False)
            k = ring["n0"]
            ring["n0"] += 1
            z0 = self.zring[k % len(self.zring)]
            idx = (k % NE, k % NSP, k % NEX, k % NA)
            ei, si, xi, ai = idx
            di = kb - 4 * c
            c0 = self.col0(kb)
            for hh in range(2):
                base = hh * 64
                zb = z0 + hh
                mm(zb, banks[zb][:, c0:512], KTh[hbuf][base:base + 64, kb * 128:(kb + 1) * 128],
                   self.QT[base:base + 64, pair, c0:512], True, di < 0, [KTh_t[hbuf], self.QT_t],
                   sig=(di < 0 and hh == 1))
                if di >= 0:
                    dc = 128 * di
                    mm(zb, banks[zb][:, dc:dc + 128], ident[:], negm[:], False, True, [cst_t], sig=(hh == 1))
            zz = psum_all[:, z0 * 512:(z0 + 2) * 512]
            act(self.v2(E2[ei][:], c0), self.v2(zz, c0), AF.Exp, [bank_t[z0], bank_t[z0 + 1]], [E2_t[ei]],
                scale=0.125)
            act(self.v2(SP2[si][:], c0), self.v2(E2[ei][:], c0), AF.Ln, [E2_t[ei]], [SP2_t[si]], bias=1.0)
            return idx

        def s_tri(self, i, idx):
            pair, kb = self.steps[i]
            ei, si, xi, ai = idx
            c0 = self.col0(kb)
            for hh in range(2):
                mm(hh, banks[hh][:, c0:512], tri[:], SP2[si][:, hh * 512 + c0:(hh + 1) * 512], kb == self.nkb - 1,
                   False, [cst_t, SP2_t[si]], sig=(hh == 1))
            act(self.v2(EX2[xi][:], c0), self.v2(psum_all[:, 0:1024], c0), AF.Exp, [bank_t[0], bank_t[1]],
                [EX2_t[xi]], scale=-1.0)

        def s_omt(self, i, idx):
            pair, kb = self.steps[i]
            ei, si, xi, ai = idx
            c0 = self.col0(kb)
            if kb > 0:
                for hh in range(2):
                    mm(hh, banks[hh][:, c0:512], omt[:], SP2[si][:, hh * 512 + c0:(hh + 1) * 512], False, False,
                       [cst_t, SP2_t[si]], sig=(hh == 1))
            tt("dve", self.v2(A2[ai][:], c0), self.v2(E2[ei][:], c0), self.v2(EX2[xi][:], c0), ALU.mult,
               [E2_t[ei], EX2_t[xi]], [A2_t[ai]])

        def s_av(self, i, idx):
            pair, kb = self.steps[i]
            ei, si, xi, ai = idx
            hbuf = pair % 2
            ob = self.obanks[pair % len(self.obanks)]
            c0 = self.col0(kb)
            for hh in range(2):
                base = hh * 64
                mm(ob, banks[ob][base:base + 64, c0:512], Vh[hbuf][:, kb, base:base + 64],
                   A2[ai][:, hh * 512 + c0:(hh + 1) * 512], kb == self.nkb - 1, kb == 0, [Vh_t[hbuf], A2_t[ai]],
                   sig=(hh == 1))
            if kb == 0:
                evac(self.OT[:, pair, :], banks[ob][:], [bank_t[ob]], [self.OT_t])

        def step(self, allow_new):
            it = self.it
            if allow_new and self.nxt < len(self.steps):
                i = self.nxt
                self.nxt += 1
                self.fly[i] = (it, self.s_z(i))
            for i in sorted(self.fly):
                t0, idx = self.fly[i]
                if it - t0 == 2:
                    self.s_omt(i, idx)
            for i in sorted(self.fly):
                t0, idx = self.fly[i]
                if it - t0 == 1:
                    self.s_tri(i, idx)
            for i in sorted(self.fly):
                t0, idx = self.fly[i]
                if it - t0 == 3:
                    self.s_av(i, idx)
                    del self.fly[i]
            self.it += 1

        def drain(self):
            while self.nxt < len(self.steps) and self.steps[self.nxt][1] != self.nkb - 1:
                self.step(True)
            while self.fly:
                self.step(False)

        def finish(self):
            while not self.done():
                self.step(True)

    CTX_Y = (xnT, xnT_t, QT, QT_t)
    CTX_X = (xnT_X, xnT_X_t, QT_X, QT_X_t)

    def shared_kv(c, hb):
        norm_T(hb, 8)
        wt = cast_weight("w_kv")
        a = wbf["w_kv"].ap()
        s = load_slab([(0, 64, a[:, 0:64]), (64, 64, a[:, 0:64]), (128, 64, a[:, 64:128]),
                       (192, 64, a[:, 64:128]), (256, 128, a[:, 128:256])], wt)
        if c > 0:
            for g in range(2):
                P.op("pool", lambda e, g=g: e.tensor_copy(KB[g][:, 0:128], KB[g][:, 512:640]),
                     reads=[KB_t[g]], writes=[KB_t[g]])
            P.op("pool", lambda e: e.tensor_copy(VB[:, 0, :, :], VB[:, 4, :, :]), reads=[VB_t], writes=[VB_t])
        for g in range(2):
            b = gbank()
            for kc in range(8):
                mm(b, banks[b][:], slab[s][:, kc, g * 128:(g + 1) * 128], xnT[:, kc, :], kc == 0, kc == 7,
                   [slab_t[s], xnT_t])
            act(KB[g][:, 128:640], banks[b][:], AF.Identity, [bank_t[b], cst_t], [KB_t[g]], bias=bk[:, g:g + 1])
        for tb in range(4):
            b = gbank()
            for kc in range(8):
                mm(b, banks[b][:, 0:128], xnT[:, kc, tb * 128:(tb + 1) * 128], slab[s][:, kc, 256:384],
                   kc == 0, kc == 7, [xnT_t, slab_t[s]])
            tt("dve", VB[:, tb + 1, :, 0:64], banks[b][:, 0:128].rearrange("p (g d) -> p g d", g=2),
               bv[:].rearrange("p (g d) -> p g d", g=2), ALU.add, [bank_t[b], cst_t], [VB_t])

    def layer_b(c, hb, j):
        norm_T(hb, 6 + j)
        for s2 in range(2):
            s = wslab("b_wq", j, 0, s2 * 512)
            for pp in range(4):
                pair = s2 * 4 + pp
                proj_featmajor(s, pp * 128, QT, QT_t, pair, bias=bq[:, j, pair:pair + 1])

        def parts_of(tb):
            return [1] + ([0] if 4 * c + tb > 0 else [])

        def stage_a(tb):
            PTb, PTb_t = PT[tb % len(PT)], PT_t[tb % len(PT)]
            for part in parts_of(tb):
                kc0 = (tb + part) * 128
                for hg in range(4):
                    g8, par = hg // 2, hg % 2
                    base = par * 64
                    b = 5 + par
                    b2 = 3 + par
                    for hi in range(4):
                        hd = g8 * 8 + par + 2 * hi
                        mm(b, banks[b][:, hi * 128:(hi + 1) * 128], KB[g8][base:base + 64, kc0:kc0 + 128],
                           QT[base:base + 64, hd // 2, tb * 128:(tb + 1) * 128], True, True,
                           [KB_t[g8], QT_t], sig=(hi == 3))
                    h0 = g8 * 8 + par
                    act(banks[b2][:], banks[b][:], AF.Exp, [bank_t[b]], [bank_t[b2]], scale=0.125)
                    tt("dve", PTb[:, part, h0:h0 + 7:2, :], banks[b2][:].rearrange("p (a t) -> p a t", a=4),
                       EB[:, part, h0:h0 + 7:2, :], ALU.mult, [bank_t[b2], EB_t], [PTb_t])

        def stage_b(tb):
            PTb, PTb_t = PT[tb % len(PT)], PT_t[tb % len(PT)]
            parts = parts_of(tb)
            for hd in range(16):
                g = hd // 8
                ob = hd // 7
                oc = (hd % 7) * 66
                for pi, part in enumerate(parts):
                    mm(ob, banks[ob][:, oc:oc + 66], PTb[:, part, hd, :], VB[:, tb + part, g, 0:66],
                       pi == 0, pi == len(parts) - 1, [PTb_t, VB_t],
                       sig=(pi == len(parts) - 1 and (hd % 7 == 6 or hd == 15)))
            for ob in range(3):
                nh = 7 if ob < 2 else 2
                bv3 = banks[ob][:, 0:nh * 66].rearrange("p (a d) -> p a d", d=66)
                tt("dve", den[:, 7 * ob:7 * ob + nh], bv3[:, :, 64], ES[:, j * 16 + 7 * ob:j * 16 + 7 * ob + nh],
                   ALU.add, [bank_t[ob], cst_t], [den_t])
            P.op("dve", lambda e: e.reciprocal(den[:, 16:32], den[:, 0:16]), reads=[den_t], writes=[den_t])
            for ob in range(3):
                nh = 7 if ob < 2 else 2
                bv3 = banks[ob][:, 0:nh * 66].rearrange("p (a d) -> p a d", d=66)
                rv = bass.AP(den, 16 + 7 * ob, [[32, 128], [1, nh], [0, 64]])
                tt("dve", Otok[:, 7 * ob * 64:(7 * ob + nh) * 64].rearrange("p (a d) -> p a d", d=64),
                   bv3[:, :, 0:64], rv, ALU.mult, [bank_t[ob], den_t], [Otok_t])
            tpv = bf_bank(7)
            for kc in range(8):
                P.op("pe", lambda e, kc=kc: e.transpose(tpv[:, kc * 128:(kc + 1) * 128],
                                                        Otok[:, kc * 128:(kc + 1) * 128], ident[:]),
                     reads=[Otok_t, cst_t], writes=[bank_t[7]], sig=(kc == 7))
            evac(OT[:, :, tb * 128:(tb + 1) * 128], tpv.rearrange("p (k t) -> p k t", k=8), [bank_t[7]], [OT_t])

        for tb in range(4):
            stage_a(tb)
            stage_b(tb)
        dma("pool", bo[:], bo_d.ap()[0][j * D:(j + 1) * D].partition_broadcast(128), "bo", [], [bo_t])
        for tb in range(4):
            P.op("pool", lambda e, tb=tb: e.tensor_tensor(h[hb][:, tb, :], h[hb][:, tb, :], bo[:], ALU.add),
                 reads=[h_t[hb], bo_t], writes=[h_t[hb]])
        resid_proj(hb, OT, OT_t, "b_wo", j)

    def final_out(c, hb, do_norm=True):
        for tb in range(4):
            stat_t = stat_ts[tb]
            if do_norm:
                P.op("dve", lambda e, tb=tb: e.scalar_tensor_tensor(junk[:], h[hb][:, tb, :], 1.0, h[hb][:, tb, :],
                                                                    ALU.mult, ALU.mult,
                                                                    accum_out=stat[:, tb:tb + 1]),
                     reads=[h_t[hb]], writes=[junk_t, stat_t])
                act(stat[:, 4 + tb:5 + tb], stat[:, tb:tb + 1], AF.Ln, [stat_t], [stat_t], scale=1.0 / D, bias=EPS)
                act(stat[:, 8 + tb:9 + tb], stat[:, 4 + tb:5 + tb], AF.Exp, [stat_t], [stat_t], scale=-0.5)
                P.op("dve", lambda e, tb=tb: e.scalar_tensor_tensor(h[hb][:, tb, :], h[hb][:, tb, :],
                                                                    stat[:, 8 + tb:9 + tb], gF[:],
                                                                    ALU.mult, ALU.mult),
                     reads=[h_t[hb], stat_t, cst_t], writes=[h_t[hb]])
                dma("pool", out_d.ap()[c * TILE + tb * 128:c * TILE + (tb + 1) * 128, :], h[hb][:, tb, :], "out",
                    [h_t[hb]], [])
            else:
                dma("pool", out_d.ap()[c * TILE + tb * 128:c * TILE + (tb + 1) * 128, :], h[hb][:, tb, :], "out",
                    [h_t[hb]], [])

    setup_consts()
    dma("pool", h[0][:], x_d.ap()[0:TILE, :].rearrange("(tb p) f -> p tb f", p=128), "xld", [], [h_t[0]])
    P.op("pool", lambda e: e.memset(VB[:], 0.0), writes=[VB_t])
    P.op("pool", lambda e: e.memset(VB[:, :, :, 64:65], 1.0), writes=[VB_t])

    def x_start(c):
        a_front(c, 0, 0, CTX_X)
        return Attn(c, 0, QT_X, QT_X_t, OT_X, OT_X_t, zring=(3,), obanks=(2,))

    def cast_all():
        for l in range(n_a):
            for nm in ("a_wqkv", "a_wo", "mlp_up", "mlp_down"):
                cast_weight(nm, l)
        for j in range(n_b):
            if j == 0:
                cast_weight("w_kv")
            cast_weight("b_wq", j)
            cast_weight("b_wo", j)
            cast_weight("mlp_up", 2 + j)
            cast_weight("mlp_down", 2 + j)

    n_mlp = n_a + n_b
    if n_a > 0:
        cast_weight("a_wqkv", 0)
        g0 = x_start(0)
        for nm in ("a_wo", "mlp_up", "mlp_down"):
            cast_weight(nm, 0)
        g0.finish()
    else:
        cast_all()
    for c in range(n_tiles):
        hb = 0
        if c > 0:
            dma("pool", h[0][:], x_d.ap()[c * TILE:(c + 1) * TILE, :].rearrange("(tb p) f -> p tb f", p=128),
                "xld", [], [h_t[0]])
        xstate["g"] = None
        if n_a > 0:
            resid_proj(hb, OT_X, OT_X_t, "a_wo", 0)
            if c + 1 < n_tiles:
                g = x_start(c + 1)
                if c == 0:
                    cast_all()
                xstate.update(g=g, acc=0.0, ratio=g.n_iters() / (0.9 * 64 * n_mlp))
            mlp(hb, 0)
            for l in range(1, n_a):
                a_front(c, l, hb, CTX_Y)
                Attn(c, l, QT, QT_t, OT, OT_t, zring=(3, 5), obanks=(2, 7)).finish()
                resid_proj(hb, OT, OT_t, "a_wo", l)
                mlp(hb, l)
        for j in range(n_b):
            if j == 0:
                shared_kv(c, hb)
            layer_b(c, hb, j)
            mlp(hb, 2 + j)
        if xstate["g"] is not None:
            xstate["g"].finish()
        final_out(c, hb, do_norm=not dbg)

    if dbg:
        print("sbuf bytes remaining", nc.sbuf_bytes_remaining)
    sem_names = ["pe", "act", "dve", "pool"]
    sems = {n: es.enter_context(nc.semaphore("sem_" + n)) for n in sem_names}
    dsems = {n: es.enter_context(nc.semaphore("dsem_" + n)) for n in P.dma_val}
    with nc.Block() as block:
        P.emit(nc, block, sems, dsems)
    es.close()
    return nc


def _bucket_onehot():
    n = np.arange(128)
    nf = np.maximum(n, 1).astype(np.float32)
    large = 16 + (np.log(nf / np.float32(16)) / np.float32(np.log(128 / 16)) * np.float32(16)).astype(np.int32)
    large = np.minimum(large, 31)
    bucket = np.where(n < 16, n, large)
    oh = np.zeros((32, 128), np.float32)
    oh[bucket, n] = 1.0
    return oh


def make_in_maps(inp, n_cores, S):
    f = lambda a: np.ascontiguousarray(np.asarray(a, dtype=np.float32))
    g_all = np.stack([inp["a_norm"][0], inp["a_norm"][1], inp["mlp_norm"][0], inp["mlp_norm"][1],
                      inp["mlp_norm"][2], inp["mlp_norm"][3], inp["b_norm"][0], inp["b_norm"][1],
                      inp["kv_norm"]], 0)
    gT = f(np.asarray(g_all).reshape(9, 8, 128).transpose(2, 0, 1))
    bqT = f(np.asarray(inp["b_bq"]).reshape(2, 8, 128).transpose(2, 0, 1))
    bkv = np.asarray(inp["b_kv"])
    bk_dup = f(np.stack([np.concatenate([bkv[0:64], bkv[0:64]]), np.concatenate([bkv[64:128], bkv[64:128]])], 1))
    shared = {
        "a_wqkv": f(inp["a_wqkv"]), "a_wo": f(inp["a_wo"]), "w_kv": f(inp["w_kv"]), "b_wq": f(inp["b_wq"]),
        "b_wo": f(inp["b_wo"]), "mlp_up": f(inp["mlp_up"]), "mlp_down": f(inp["mlp_down"]),
        "gT": gT, "gF": f(np.asarray(inp["final_norm"]).reshape(1, D)), "bqT": bqT, "bk_dup": bk_dup,
        "bv": f(bkv[128:256].reshape(1, 128)), "bo": f(np.asarray(inp["b_bo"]).reshape(1, 2 * D)),
        "sinks": f(np.asarray(inp["b_sinks"]).reshape(1, 32)), "rel_bias": f(inp["rel_bias"]),
        "onehot": _bucket_onehot(),
    }
    x = np.asarray(inp["x"], dtype=np.float32)
    maps = []
    for b in range(n_cores):
        m = dict(shared)
        m["x"] = np.ascontiguousarray(x[b, :S])
        maps.append(m)
    return maps


def kernel(**inputs):
    nc = bass.Bass("TRN2", target_bir_lowering=False)
    build(nc)
    in_maps = make_in_maps(inputs, 8, SEQ)
    res = run_bass_kernel_spmd(nc, in_maps, core_ids=list(range(8)))
    return np.stack([np.asarray(r["out"]) for r in res.results], 0).astype(np.float32)
```
